# Optimizing a Trainium2 kernel written in Bass

```python
import jax, jax.numpy as jnp
from jax import lax
import numpy as np

D_MODEL = 1024
BATCH = 32
SEQ = 2048
DEPTH = 1
DEC_BATCH = 4
DEC_SEQ = 4096
PAST_LEN = 128

GRID_W = 64
N_ATTN_HEADS = 8
HEAD_DIM = 64
D_ATTN = N_ATTN_HEADS * HEAD_DIM
D_CONV = D_MODEL - D_ATTN
N_CONV_GROUPS = 8
CONV_WIDTH = 3
NA_ROWS = 8
NA_COLS = 16
D_FF = 2816
D_IN = 3 * D_ATTN + 3 * D_CONV
EPS = 1e-6

kernel_name = "hymba_natten_shortconv_macaron_encoder"


def rms_norm(x, g):
    xf = x.astype(jnp.float32)
    y = xf * lax.rsqrt(jnp.mean(xf * xf, axis=-1, keepdims=True) + EPS)
    return (y * g.astype(jnp.float32)).astype(x.dtype)


def swiglu(x, w_gate, w_up, w_down):
    return (jax.nn.silu(x @ w_gate) * (x @ w_up)) @ w_down


def neighborhood_attention(q, k, v, rpb):
    b, s, h, dh = q.shape
    rows = s // GRID_W
    kh = min(NA_ROWS, rows)
    kw = NA_COLS
    qg = q.reshape(b, rows, GRID_W, h, dh)
    kg = k.reshape(b, rows, GRID_W, h, dh)
    vg = v.reshape(b, rows, GRID_W, h, dh)
    cols = jnp.arange(GRID_W)
    col_start = jnp.clip(cols - kw // 2, 0, GRID_W - kw)
    col_idx = col_start[:, None] + jnp.arange(kw)[None, :]
    col_off = col_idx - cols[:, None]
    bias_cols = rpb[:, :, col_off + NA_COLS - 1]
    scale = HEAD_DIM ** -0.5

    def one_row(r):
        rs = jnp.clip(r - kh // 2, 0, rows - kh)
        q_r = lax.dynamic_index_in_dim(qg, r, axis=1, keepdims=False)
        k_blk = lax.dynamic_slice_in_dim(kg, rs, kh, axis=1)
        v_blk = lax.dynamic_slice_in_dim(vg, rs, kh, axis=1)
        k_win = k_blk[:, :, col_idx]
        v_win = v_blk[:, :, col_idx]
        row_off = rs + jnp.arange(kh) - r
        bias = jnp.transpose(bias_cols[:, row_off + NA_ROWS - 1], (0, 2, 1, 3))
        sc = jnp.einsum('bchd,bicjhd->bhcij', q_r, k_win).astype(jnp.float32) * scale
        sc = sc + bias[None].astype(jnp.float32)
        p = jax.nn.softmax(sc.reshape(b, h, GRID_W, kh * kw), axis=-1)
        p = p.reshape(b, h, GRID_W, kh, kw).astype(v.dtype)
        return jnp.einsum('bhcij,bicjhd->bchd', p, v_win)

    out = lax.map(one_row, jnp.arange(rows))
    return jnp.transpose(out, (1, 0, 2, 3, 4)).reshape(b, s, h, dh)


def short_gated_conv(gb, gc, xin, conv_w):
    s = xin.shape[1]
    u = gc * xin
    half = CONV_WIDTH // 2
    up = jnp.pad(u, ((0, 0), (half, CONV_WIDTH - 1 - half), (0, 0)))
    y = up[:, 0:s] * conv_w[0]
    for j in range(1, CONV_WIDTH):
        y = y + up[:, j:j + s] * conv_w[j]
    return gb * y


def encoder_layer(x, ffn1_norm, ffn1_w_gate, ffn1_w_up, ffn1_w_down, mix_norm, w_in,
                  q_norm, k_norm, rel_pos_bias, conv_w, attn_out_norm, conv_out_norm, w_out,
                  ffn2_norm, ffn2_w_gate, ffn2_w_up, ffn2_w_down, final_norm):
    b, s, _ = x.shape
    h = x + 0.5 * swiglu(rms_norm(x, ffn1_norm), ffn1_w_gate, ffn1_w_up, ffn1_w_down)
    u = rms_norm(h, mix_norm)
    z = u @ w_in
    q, k, v, gb, gc, xin = jnp.split(
        z, [D_ATTN, 2 * D_ATTN, 3 * D_ATTN, 3 * D_ATTN + D_CONV, 3 * D_ATTN + 2 * D_CONV], axis=-1)
    q = rms_norm(q.reshape(b, s, N_ATTN_HEADS, HEAD_DIM), q_norm)
    k = rms_norm(k.reshape(b, s, N_ATTN_HEADS, HEAD_DIM), k_norm)
    v = v.reshape(b, s, N_ATTN_HEADS, HEAD_DIM)
    a = neighborhood_attention(q, k, v, rel_pos_bias)
    a = rms_norm(a, attn_out_norm.reshape(N_ATTN_HEADS, HEAD_DIM)).reshape(b, s, D_ATTN)
    c = short_gated_conv(gb, gc, xin, conv_w)
    gdim = D_CONV // N_CONV_GROUPS
    c = rms_norm(c.reshape(b, s, N_CONV_GROUPS, gdim),
                 conv_out_norm.reshape(N_CONV_GROUPS, gdim)).reshape(b, s, D_CONV)
    h = h + jnp.concatenate([a, c], axis=-1) @ w_out
    h = h + 0.5 * swiglu(rms_norm(h, ffn2_norm), ffn2_w_gate, ffn2_w_up, ffn2_w_down)
    return rms_norm(h, final_norm)


def setup_inputs(seed: int = 0) -> dict:
    key = jax.random.key(seed)
    ks = jax.random.split(key, 24)

    def w(k, shape, fan_in):
        return jax.random.normal(k, shape, jnp.float32) * (fan_in ** -0.5)

    def gain(k, shape):
        return 1.0 + 0.02 * jax.random.normal(k, shape, jnp.float32)

    L = DEPTH
    return {
        "x_prompt": jax.random.normal(ks[0], (BATCH, SEQ, D_MODEL), jnp.float32),
        "x_sample": jax.random.normal(ks[1], (DEC_BATCH, DEC_SEQ, D_MODEL), jnp.float32),
        "ffn1_norm": gain(ks[2], (L, D_MODEL)),
        "ffn1_w_gate": w(ks[3], (L, D_MODEL, D_FF), D_MODEL),
        "ffn1_w_up": w(ks[4], (L, D_MODEL, D_FF), D_MODEL),
        "ffn1_w_down": w(ks[5], (L, D_FF, D_MODEL), D_FF),
        "mix_norm": gain(ks[6], (L, D_MODEL)),
        "w_in": w(ks[7], (L, D_MODEL, D_IN), D_MODEL),
        "q_norm": gain(ks[8], (L, HEAD_DIM)),
        "k_norm": gain(ks[9], (L, HEAD_DIM)),
        "rel_pos_bias": 0.1 * jax.random.normal(ks[10], (L, N_ATTN_HEADS, 2 * NA_ROWS - 1, 2 * NA_COLS - 1), jnp.float32),
        "conv_w": w(ks[11], (L, CONV_WIDTH, D_CONV), CONV_WIDTH),
        "attn_out_norm": gain(ks[12], (L, D_ATTN)),
        "conv_out_norm": gain(ks[13], (L, D_CONV)),
        "w_out": w(ks[14], (L, D_MODEL, D_MODEL), D_MODEL),
        "ffn2_norm": gain(ks[15], (L, D_MODEL)),
        "ffn2_w_gate": w(ks[16], (L, D_MODEL, D_FF), D_MODEL),
        "ffn2_w_up": w(ks[17], (L, D_MODEL, D_FF), D_MODEL),
        "ffn2_w_down": w(ks[18], (L, D_FF, D_MODEL), D_FF),
        "final_norm": gain(ks[19], (L, D_MODEL)),
    }


def reference(x_prompt, x_sample, ffn1_norm, ffn1_w_gate, ffn1_w_up, ffn1_w_down, mix_norm, w_in,
              q_norm, k_norm, rel_pos_bias, conv_w, attn_out_norm, conv_out_norm, w_out,
              ffn2_norm, ffn2_w_gate, ffn2_w_up, ffn2_w_down, final_norm):
    y_prompt = x_prompt
    y_sample = x_sample
    for l in range(DEPTH):
        p = (ffn1_norm[l], ffn1_w_gate[l], ffn1_w_up[l], ffn1_w_down[l], mix_norm[l], w_in[l],
             q_norm[l], k_norm[l], rel_pos_bias[l], conv_w[l], attn_out_norm[l], conv_out_norm[l], w_out[l],
             ffn2_norm[l], ffn2_w_gate[l], ffn2_w_up[l], ffn2_w_down[l], final_norm[l])
        y_prompt = encoder_layer(y_prompt, *p)
        y_sample = encoder_layer(y_sample, *p)
    return (y_prompt, y_sample)
```

```python
import numpy as np
from contextlib import ExitStack
import concourse.bass as bass
import concourse.mybir as mybir
from concourse.bass_utils import run_bass_kernel_spmd

F32 = mybir.dt.float32
BF16 = mybir.dt.bfloat16
AF = mybir.ActivationFunctionType
ALU = mybir.AluOpType
AX = mybir.AxisListType

D = 1024
FF = 2816
NF = 22
T = 1024
NTB = T // 128
NEG = -30000.0
EPS = 1e-6
QUARTERS = [(0, 3), (3, 6), (6, 10), (10, 14), (14, 18), (18, 22)]
NFQ = 4
NYY = 14


def _unit_struct(kind):
    if kind == "prompt":
        R_ext, own0 = 32, 0
        ws = [lambda e: min(max(e - 4, 0), 24)] * 2
        exist = [lambda kr: 0 <= kr < 32] * 2
    else:
        R_ext, own0 = 40, 4
        ws = [lambda e: max(e - 4, 4), lambda e: min(e - 4, 28)]
        exist = [lambda kr: 4 <= kr < 40, lambda kr: 0 <= kr < 36]
    return R_ext, own0, ws, exist


def build_attn_plan():
    static_cols = {(True, True): 0, (True, False): 1, (False, True): 2, (False, False): 3}
    dyn_cols = []
    plans = {}
    for kind in ("prompt", "sample"):
        R_ext, own0, ws, exist = _unit_struct(kind)
        qps = []
        for q in range(16):
            e0 = own0 + 2 * q
            rows = set()
            for v in range(2):
                for j in range(2):
                    s = ws[v](e0 + j)
                    rows.update(range(s, s + 8))
            kp_lo, kp_hi = min(rows) // 2, max(rows) // 2
            combos = list(range(kp_hi, kp_lo - 1, -1))
            runs = []
            for ci, kp in enumerate(combos):
                for j in range(2):
                    pats = []
                    for v in range(2):
                        s = ws[v](e0 + j)
                        pat = tuple((s <= 2 * kp + i < s + 8) and exist[v](2 * kp + i) for i in range(2))
                        pats.append(pat)
                    if pats[0] == pats[1]:
                        col = static_cols[pats[0]]
                    else:
                        dyn_cols.append((pats[0], pats[1]))
                        col = 4 + len(dyn_cols) - 1
                    c0 = ci * 128 + j * 64
                    if runs and runs[-1][2] == col and col < 4 and runs[-1][1] == c0:
                        runs[-1] = (runs[-1][0], c0 + 64, col)
                    else:
                        runs.append((c0, c0 + 64, col))
            qp_ext = e0 // 2
            dmax = 2 * (combos[0] - qp_ext)
            y0 = 8 - dmax
            assert 2 <= y0 and y0 + 2 * len(combos) <= 16, (kind, q, y0, combos)
            qps.append(dict(qp_ext=qp_ext, combos=combos, runs=runs, y0=y0))
        plans[kind] = qps
    return plans, dyn_cols


def mask_table(dyn_cols, variant):
    nm = 4 + len(dyn_cols)
    m = np.zeros((128, nm), np.float32)
    m[64:, 1] = NEG
    m[:64, 2] = NEG
    m[:, 3] = NEG
    for k, pats in enumerate(dyn_cols):
        p = pats[variant]
        if not p[0]:
            m[:64, 4 + k] = NEG
        if not p[1]:
            m[64:, 4 + k] = NEG
    return m


class Res:
    __slots__ = ("w", "r", "name")

    def __init__(self, name=""):
        self.w = None
        self.r = {}
        self.name = name


class EngQ:
    def __init__(self, nc, es, eng, name):
        self.eng = eng
        self.sem = es.enter_context(nc.semaphore("s_" + name))
        self.count = 0
        self.waited = {}
        self.name = name

    def wait(self, tok):
        if tok is None:
            return
        sem, val = tok
        k = id(sem)
        if self.waited.get(k, 0) >= val:
            return
        self.eng.wait_ge(sem, val)
        self.waited[k] = val

    def signal(self, inst):
        self.count += 1
        inst.then_inc(self.sem, 1)
        return (self.sem, self.count)

    def cur(self):
        return (self.sem, self.count) if self.count else None


class Stream:
    def __init__(self, nc, es, name):
        self.sem = es.enter_context(nc.semaphore("d_" + name))
        self.n = 0

    def cur(self):
        return (self.sem, 16 * self.n) if self.n else None


def _deps(q, reads, writes):
    toks = []
    own = q.sem
    for r in reads:
        if r.w is not None and not (r.w[0] is own and q.name == "pe"):
            toks.append(r.w)
    for w in writes:
        if w.w is not None and w.w[0] is not own:
            toks.append(w.w)
        for t in w.r.values():
            if t[0] is not own:
                toks.append(t)
    return toks


def _commit(tok, reads, writes):
    for r in reads:
        k = id(tok[0])
        old = r.r.get(k)
        if old is None or old[1] < tok[1]:
            r.r[k] = tok
    for w in writes:
        w.w = tok
        w.r = {}


def op(q, fn, reads=(), writes=()):
    for t in _deps(q, reads, writes):
        q.wait(t)
    inst = fn()
    tok = q.signal(inst)
    _commit(tok, reads, writes)
    return tok


def pe_group(q, fns, reads=(), writes=()):
    for t in _deps(q, reads, writes):
        q.wait(t)
    inst = None
    for fn in fns:
        inst = fn()
    tok = q.signal(inst)
    _commit(tok, reads, writes)
    return tok


def pe_multi(q, groups):
    for fns, reads, writes in groups:
        for t in _deps(q, reads, writes):
            q.wait(t)
    for fns, reads, writes in groups:
        inst = None
        for fn in fns:
            inst = fn()
        tok = q.signal(inst)
        _commit(tok, reads, writes)


def dma(q, stream, out, in_, reads=(), writes=(), **kw):
    q.wait(stream.cur())
    for t in _deps(q, reads, writes):
        q.wait(t)
    inst = q.eng.dma_start(out=out, in_=in_, **kw)
    stream.n += 1
    inst.then_inc(stream.sem, 16)
    tok = (stream.sem, 16 * stream.n)
    _commit(tok, reads, writes)
    return tok


def build_program(n_prompt=4, with_sample=True):
    plans, dyn_cols = build_attn_plan()
    NM = 4 + len(dyn_cols)
    units = []
    tok_in = 0
    for u in range(n_prompt):
        units.append(dict(kind="prompt", in0=tok_in, n_ext=2048, own_off=0, y0=u * 2048))
        tok_in += 2048
    if with_sample:
        units.append(dict(kind="sample", in0=tok_in, n_ext=2560, own_off=256, y0=n_prompt * 2048))
        tok_in += 2560
    NTOK_IN = tok_in
    NTOK_OUT = len(units) * 2048

    nc = bass.Bass("TRN2", target_bir_lowering=False)

    def din(name, shape, dt=F32):
        return nc.dram_tensor(name, list(shape), dt, kind="ExternalInput")

    def dscr(name, shape, dt):
        return nc.dram_tensor(name, list(shape), dt, kind="Internal")

    xin = din("xin", [NTOK_IN, D]).ap()
    yout = nc.dram_tensor("yout", [NTOK_OUT, D], F32, kind="ExternalOutput").ap()
    g1_d = din("g1", [128, 8]).ap()
    g2_d = din("g2", [128, 8]).ap()
    gm_d = din("gm", [128, 8]).ap()
    go_d = din("go", [128, 8]).ap()
    gq_d = din("gq", [128, 1]).ap()
    gk_d = din("gk", [128, 1]).ap()
    cw_d = din("convw", [128, 12]).ap()
    gfin_d = din("gfin", [128, D]).ap()
    w1g_d = din("w1g", [D, FF]).ap()
    w1u_d = din("w1u", [D, FF]).ap()
    w1d_d = din("w1d", [FF, D]).ap()
    win_d = din("win", [D, 3072]).ap()
    wout_d = din("wout", [D, D]).ap()
    w2g_d = din("w2g", [D, FF]).ap()
    w2u_d = din("w2u", [D, FF]).ap()
    w2d_d = din("w2d", [FF, D]).ap()
    rpbr_d = din("rpb_rev", [120, 31]).ap()
    ident_d = din("ident", [128, 128]).ap()
    ones_d = din("onesblk", [128, 128]).ap()
    colmask_d = din("colmask", [128, 64]).ap()
    maskt_d = din("maskt", [128, NM]).ap()
    mask01_d = din("mask01", [128, NM]).ap()

    W1g_s = dscr("W1g_s", [NF, 128, 1024], BF16)
    W1u_s = dscr("W1u_s", [NF, 128, 1024], BF16)
    W2g_s = dscr("W2g_s", [NF, 128, 1024], BF16)
    W2u_s = dscr("W2u_s", [NF, 128, 1024], BF16)
    W1d_s = dscr("W1d_s", [NF, 128, 1024], BF16)
    W2d_s = dscr("W2d_s", [NF, 128, 1024], BF16)
    Win_s = dscr("Win_s", [24, 128, 1024], BF16)
    Wv_s = dscr("Wv_s", [128, 8, 512], BF16)
    Wout_s = dscr("Wout_s", [128, 8, 1024], BF16)
    h_s = dscr("h_s", [2048, D], F32)
    rpbp = dscr("rpbp", [120, 160], F32)
    rpbsk = dscr("rpbsk", [120, 64, 64], F32)

    with ExitStack() as es:
        pe = EngQ(nc, es, nc.tensor, "pe")
        act = EngQ(nc, es, nc.scalar, "act")
        dve = EngQ(nc, es, nc.vector, "dve")
        pool = EngQ(nc, es, nc.gpsimd, "pool")
        sp = EngQ(nc, es, nc.sync, "sp")
        engs = [pe, act, dve, pool, sp]
        sx = [Stream(nc, es, f"x{i}") for i in range(NTB)]
        sxs = [Stream(nc, es, f"xs{i}") for i in range(NTB)]
        sw = [[Stream(nc, es, f"w{i}_{j}") for j in range(2)] for i in range(3)]
        swd = [Stream(nc, es, f"wd{i}") for i in range(2)]
        st_su = Stream(nc, es, "su")
        su_pool = [Stream(nc, es, f"su{i}") for i in range(32)]
        su_i = [0]

        def su_next():
            st = su_pool[su_i[0] % len(su_pool)]
            su_i[0] += 1
            return st
        s_wrow = [Stream(nc, es, f"wrow{i}") for i in range(2)]
        s_stage = [Stream(nc, es, f"stage{i}") for i in range(2)]
        streams = sx + sxs + sw[0] + sw[1] + sw[2] + swd + [st_su] + su_pool + s_wrow + s_stage

        def barrier():
            toks = [e.cur() for e in engs if e is not sp] + [s.cur() for s in streams]
            for e in engs:
                for t in toks:
                    if t is not None and t[0] is not e.sem:
                        e.wait(t)

        def sb(name, shape, dt=F32):
            return es.enter_context(nc.sbuf_tensor("sb_" + name, list(shape), dt))

        ident = sb("ident", [128, 128], BF16)
        onesb = sb("onesb", [128, 128], BF16)
        g_sb = sb("g_sb", [128, 4, 8])
        gq_sb = sb("gq_sb", [128, 1])
        gk_sb = sb("gk_sb", [128, 1])
        cw_sb = sb("cw_sb", [128, 12])
        gfin_sb = sb("gfin_sb", [128, D])
        eps_sb = sb("eps_sb", [128, 1])
        maskt = sb("maskt", [128, NM])
        mask01 = sb("mask01", [128, NM])
        TT3 = sb("TT3", [128, 8, NYY * 64], BF16)
        r_const = Res("const")

        with ExitStack() as es2:
            def sb2(name, shape, dt=F32):
                return es2.enter_context(nc.sbuf_tensor("s2_" + name, list(shape), dt))
            tmpf = sb2("tmpf", [128, 256])
            zt = sb2("zt", [128, 160])
            colmask = sb2("colmask", [128, 64])
            TT3f = sb2("TT3f", [128, 8, NYY * 64])
            wrow = [sb2(f"wrow{i}", [128, 3072]) for i in range(2)]
            stage = [sb2(f"stage{i}", [128, 24 * 1024], BF16) for i in range(2)]
            r_wrow = [Res(), Res()]
            r_stage = [Res(), Res()]
            r_tmp = Res()

            r_params = []
            for dst, src in ((g_sb[:, 0, :], g1_d), (g_sb[:, 1, :], g2_d), (g_sb[:, 2, :], gm_d),
                             (g_sb[:, 3, :], go_d), (gq_sb[:], gq_d), (gk_sb[:], gk_d), (cw_sb[:], cw_d),
                             (gfin_sb[:], gfin_d), (maskt[:], maskt_d), (mask01[:], mask01_d), (colmask[:], colmask_d),
                             (tmpf[:, 0:128], ident_d), (tmpf[:, 128:256], ones_d)):
                rp_ = Res()
                r_params.append(rp_)
                dma(sp, su_next(), dst, src, writes=[rp_])
            op(dve, lambda: nc.vector.tensor_copy(out=ident[:], in_=tmpf[:, 0:128]), reads=r_params, writes=[r_tmp])
            op(dve, lambda: nc.vector.tensor_copy(out=onesb[:], in_=tmpf[:, 128:256]), reads=r_params, writes=[r_tmp])
            op(dve, lambda: nc.vector.tensor_scalar(out=gq_sb[:], in0=gq_sb[:], scalar1=0.125, scalar2=None, op0=ALU.mult),
               reads=r_params, writes=[r_tmp])
            op(dve, lambda: nc.vector.memset(eps_sb[:], EPS), writes=[r_tmp])
            op(dve, lambda: nc.vector.memset(zt[:], 0.0), writes=[r_tmp])
            r_rp = Res()
            dma(sp, st_su, rpbp.ap(), zt[0:120, :], reads=[r_tmp], writes=[r_rp])
            dma(sp, st_su, rpbp.ap()[:, 64:95], rpbr_d, reads=[], writes=[r_rp])
            r_sk = []
            for r3 in range(0, 120, 40):
                srcsk = bass.AP(tensor=rpbp, offset=r3 * 160 + 79, ap=[[160, 40], [-1, 64], [1, 64]])
                rs_ = Res()
                r_sk.append(rs_)
                dma(sp, su_next(), rpbsk.ap()[r3:r3 + 40], srcsk, reads=[r_rp], writes=[rs_])
            r_tt = {}
            for h in range(8):
                for i in range(2):
                    src = bass.AP(tensor=rpbsk, offset=(h * 15 + i + 13) * 4096,
                                  ap=[[64, 64], [-4096, NYY], [1, 64]])
                    dst = TT3f[i * 64:(i + 1) * 64, h, :].rearrange("p (y c) -> p y c", c=64)
                    r_tt[(h, i)] = Res()
                    dma(sp, su_next(), dst, src, reads=r_sk, writes=[r_tt[(h, i)]])
            for h in range(8):
                op(dve, lambda h=h: nc.vector.tensor_tensor(
                    out=TT3[:, h, :].rearrange("p (y c) -> p y c", c=64),
                    in0=TT3f[:, h, :].rearrange("p (y c) -> p y c", c=64),
                    in1=colmask[:].unsqueeze(1).broadcast_to([128, NYY, 64]), op=ALU.add),
                   reads=[r_tt[(h, 0)], r_tt[(h, 1)]] + r_params, writes=[r_tmp])

            cast_i = [0]

            def cast(out_ap, in_ap, gain_ap, reads, writes):
                i = cast_i[0]
                cast_i[0] += 1
                if i % 2 == 0:
                    if gain_ap is None:
                        op(act, lambda: nc.scalar.copy(out=out_ap, in_=in_ap), reads=reads, writes=writes)
                    else:
                        op(act, lambda: nc.scalar.activation(out=out_ap, in_=in_ap, func=AF.Identity, scale=gain_ap),
                           reads=reads + r_params, writes=writes)
                else:
                    if gain_ap is None:
                        op(dve, lambda: nc.vector.tensor_copy(out=out_ap, in_=in_ap), reads=reads, writes=writes)
                    else:
                        op(dve, lambda: nc.vector.tensor_scalar(out=out_ap, in0=in_ap, scalar1=gain_ap, scalar2=None,
                                                                op0=ALU.mult), reads=reads + r_params, writes=writes)

            mat_i = [0]
            row_i = [0]

            def prep_colmajor(src, ncols, gidx, dst_s, with_v=False):
                si = mat_i[0] % 2
                mat_i[0] += 1
                nt = ncols // 128
                stg = stage[si][:, 0:nt * 1024].rearrange("p (n k m) -> p n k m", k=8, m=128)
                for k in range(8):
                    ri = row_i[0] % 2
                    row_i[0] += 1
                    dma(sp, s_wrow[ri], wrow[ri][:, 0:ncols], src[k * 128:(k + 1) * 128, :], writes=[r_wrow[ri]])
                    cast(stg[:, :, k, :], wrow[ri][:, 0:ncols].rearrange("p (n m) -> p n m", m=128),
                         g_sb[:, gidx, k:k + 1], [r_wrow[ri]], [r_stage[si]])
                dma(pool, s_stage[si], dst_s.ap().rearrange("n p c -> p n c"), stage[si][:, 0:nt * 1024].rearrange("p (n c) -> p n c", c=1024),
                    reads=[r_stage[si]])
                if with_v:
                    for n in range(4):
                        dma(pool, s_stage[si], Wv_s.ap()[:, :, n * 128:(n + 1) * 128], stg[:, 8 + n, :, :], reads=[r_stage[si]])

            def prep_rowmajor(src, nrt, gidx, dst_ap):
                si = mat_i[0] % 2
                mat_i[0] += 1
                for f in range(nrt):
                    ri = row_i[0] % 2
                    row_i[0] += 1
                    dma(sp, s_wrow[ri], wrow[ri][:, 0:1024], src[f * 128:(f + 1) * 128, :], writes=[r_wrow[ri]])
                    cast(stage[si][:, f * 1024:(f + 1) * 1024], wrow[ri][:, 0:1024],
                         None if gidx is None else g_sb[:, gidx, f:f + 1], [r_wrow[ri]], [r_stage[si]])
                dma(pool, s_stage[si], dst_ap, stage[si][:, 0:nrt * 1024].rearrange("p (n c) -> p n c", c=1024),
                    reads=[r_stage[si]])

            prep_colmajor(w1g_d, FF, 0, W1g_s)
            prep_colmajor(w1u_d, FF, 0, W1u_s)
            prep_rowmajor(w1d_d, NF, None, W1d_s.ap().rearrange("n p c -> p n c"))
            prep_colmajor(win_d, 3072, 2, Win_s, with_v=True)
            prep_rowmajor(wout_d, 8, 3, Wout_s.ap())
            prep_colmajor(w2g_d, FF, 1, W2g_s)
            prep_colmajor(w2u_d, FF, 1, W2u_s)
            prep_rowmajor(w2d_d, NF, None, W2d_s.ap().rearrange("n p c -> p n c"))
            barrier()

        xh = sb("xh", [128, NTB, D])
        xnT = sb("xnT", [128, 8, T], BF16)
        xnb = [sb(f"xnb{i}", [128, D], BF16) for i in range(3)]
        actT = sb("actT", [128, NFQ, T], BF16)
        Wd = [sb(f"Wd{i}", [128, NFQ, D], BF16) for i in range(2)]
        Wst = [sb(f"Wst{i}", [128, 2, 1024], BF16) for i in range(3)]
        sg = [sb(f"sg{i}", [128, 512]) for i in range(2)]
        QT = sb("QT", [128, 4, 2048], BF16)
        KTt = sb("KTt", [128, 4, 2560], BF16)
        V = sb("V", [128, 20, 520], BF16)
        cT = sb("cT", [128, 4, 2048], BF16)
        Ucar = sb("Ucar", [128, 4, 2])
        Bcar = sb("Bcar", [128, 4, 1])
        st8 = sb("st8", [128, 64])
        junk = sb("junk", [128, D], BF16)
        arena = sb("arena", [128, 4224])
        Sb = [arena[:, 0:768], arena[:, 768:1536], arena[:, 1536:2304]]
        PT = [arena[:, 2304:2688].bitcast(BF16), arena[:, 2688:3072].bitcast(BF16), arena[:, 3072:3456].bitcast(BF16)]
        atok = arena[:, 3456:3968]
        abf = arena[:, 3968:4224].bitcast(BF16)
        Csb = arena[:, 0:512]
        Ub = arena[:, 512:1026]
        Bb = arena[:, 1026:1539]
        yb = arena[:, 1540:2052]
        rst = arena[:, 2052:2564]
        sqb = [arena[:, 2564:2820].bitcast(BF16), arena[:, 2820:3076].bitcast(BF16)]

        PSALL = es.enter_context(nc.psum_tensor("PSALL", [128, 4096], F32))
        BK = [PSALL[:, i * 512:(i + 1) * 512] for i in range(8)]
        PD = [BK[4], BK[5]]
        TP = [BK[6].bitcast(BF16), BK[7].bitcast(BF16)]

        r_xhh = [[Res(f"xh{t}a"), Res(f"xh{t}b")] for t in range(NTB)]
        r_xh = [None] * NTB
        r_xnT = [Res(f"xnT{t}") for t in range(NTB)]
        r_xnb = [Res(), Res(), Res()]
        r_actT = [[Res(), Res()] for _ in range(NFQ)]
        r_Wd = [Res(), Res()]
        r_Wst = [[Res(), Res()] for _ in range(3)]
        r_sg = [Res(), Res()]
        r_QT, r_KT, r_V, r_cT = Res(), Res(), Res(), Res()
        r_sqbX, r_rst, r_Csb, r_Ub, r_Bb, r_yb, r_car = Res(), Res(), Res(), Res(), Res(), Res(), Res()
        r_sqb = [Res(), Res()]
        r_Sb = [Res(), Res(), Res()]
        r_PT = [Res(), Res(), Res()]
        r_atok, r_abf, r_st8, r_junk, r_st8b = Res(), Res(), Res(), Res(), Res()
        r_ss = [Res() for _ in range(NTB)]
        r_fs = [Res() for _ in range(NTB)]
        r_BK = [Res(f"bank{i}") for i in range(8)]
        r_PS = [[r_BK[0], r_BK[1]], [r_BK[2], r_BK[3]]]
        r_PD = [r_BK[4], r_BK[5]]
        r_TP = [r_BK[6], r_BK[7]]
        r_hs = [Res() for _ in range(16)]
        PSh = [[BK[0], BK[1]], [BK[2], BK[3]]]
        SLAB = [PSALL[:, 0:1024], PSALL[:, 1024:2048], PSALL[:, 2048:3072]]
        r_SLAB = [[r_BK[0], r_BK[1]], [r_BK[2], r_BK[3]], [r_BK[4], r_BK[5]]]
        banks8 = [(BK[i], r_BK[i]) for i in range(8)]

        op(pool, lambda: nc.gpsimd.memset(V[:].rearrange("p t (h e) -> p t h e", e=65)[:, :, :, 64:65], 1.0), writes=[r_V])

        cnt = dict(tp=0, pd=0, ps=0, ps3=0, wst=0, xnb=0, sg=0, ev=0, bk=0, sq=0)

        free_banks = list(range(8))

        def alloc_bank():
            i = free_banks.pop(0)
            return BK[i], r_BK[i], i

        def release_bank(i):
            free_banks.append(i)

        def fence():
            qs = [pe, act, dve]
            toks = [e.cur() for e in qs]
            for e in qs:
                for t in toks:
                    if t is not None and t[0] is not e.sem:
                        e.wait(t)

        def evac_eng():
            cnt["ev"] += 1
            return act if cnt["ev"] % 2 == 0 else dve

        def copy_on(q, out_ap, in_ap, reads, writes):
            if q is act:
                return op(act, lambda: nc.scalar.copy(out=out_ap, in_=in_ap), reads=reads, writes=writes)
            return op(dve, lambda: nc.vector.tensor_copy(out=out_ap, in_=in_ap), reads=reads, writes=writes)

        def rstd_small(ss_ap, out_ap, scale, res):
            op(act, lambda: nc.scalar.activation(out=out_ap, in_=ss_ap, func=AF.Ln, bias=eps_sb[:], scale=scale),
               reads=[res, r_const], writes=[res])
            op(act, lambda: nc.scalar.activation(out=out_ap, in_=out_ap, func=AF.Exp, scale=-0.5),
               reads=[res], writes=[res])

        class PostPipe:
            def __init__(self):
                self.b = []
                self.c = []

            def step(self, stages):
                A, B, C = stages
                if A is not None:
                    A()
                if self.b:
                    f = self.b.pop(0)
                    if f is not None:
                        f()
                self.b.append(B)
                if len(self.c) >= 2:
                    f = self.c.pop(0)
                    if f is not None:
                        f()
                self.c.append(C)

            def flush(self):
                for f in self.b:
                    if f is not None:
                        f()
                self.b = []
                for f in self.c:
                    if f is not None:
                        f()
                self.c = []

        def norm_stages(t):
            box = {}

            def A():
                op(act, lambda: nc.scalar.activation(out=junk[:], in_=xh[:, t, :], func=AF.Square, accum_out=st8[:, t:t + 1]),
                   reads=r_xhh[t], writes=[r_junk, r_ss[t]])
                rstd_small(st8[:, t:t + 1], st8[:, 8 + t:9 + t], 1.0 / D, r_ss[t])

            def B():
                bi = cnt["xnb"] % 3
                cnt["xnb"] += 1
                box["bi"] = bi
                op(dve, lambda: nc.vector.tensor_scalar(out=xnb[bi][:], in0=xh[:, t, :], scalar1=st8[:, 8 + t:9 + t], scalar2=None,
                                                        op0=ALU.mult), reads=r_xhh[t] + [r_ss[t]], writes=[r_xnb[bi]])

            def C():
                bi = box["bi"]
                ti = cnt["tp"] % 2
                cnt["tp"] += 1
                tpv = TP[ti].rearrange("p (k m) -> p k m", m=128)
                pe_group(pe, [lambda k=k: nc.tensor.transpose(tpv[:, k, :], xnb[bi][:, k * 128:(k + 1) * 128], ident[:])
                              for k in range(8)], reads=[r_xnb[bi], r_const], writes=[r_TP[ti]])
                copy_on(evac_eng(), xnT[:, :, t * 128:(t + 1) * 128], tpv, [r_TP[ti]], [r_xnT[t]])
            return A, B, C

        def norm_tiles(ntiles):
            pp = PostPipe()
            for t in range(ntiles):
                pp.step(norm_stages(t))
            pp.flush()

        def wst_load(src_ap):
            slot = (cnt["wst"] // 2) % 3
            hf = cnt["wst"] % 2
            cnt["wst"] += 1
            dma(sp, sw[slot][hf], Wst[slot][:, hf, :], src_ap, writes=[r_Wst[slot][hf]])
            return slot, hf

        pref = []

        def get_tile(key, src_ap):
            if pref:
                k, slot, hf = pref.pop(0)
                assert k == key, (k, key)
                return slot, hf
            return wst_load(src_ap)

        def prefetch_ffn(Wg_s, Wu_s):
            for f in range(2):
                pref.append(((id(Wg_s), f),) + wst_load(Wg_s.ap()[f]))
                pref.append(((id(Wu_s), f),) + wst_load(Wu_s.ap()[f]))

        WIN_ORDER = list(range(0, 8)) + [12, 16, 20, 13, 17, 21, 14, 18, 22, 15, 19, 23]

        def prefetch_win():
            for i in range(4):
                pref.append((("win", i),) + wst_load(Win_s.ap()[WIN_ORDER[i]]))

        def ffn(Wg_s, Wu_s, Wd_s, ntiles, post_tile, after_gateup=None):
            ntok = ntiles * 128
            rx = r_xnT[0:ntiles]
            loads = {}
            pp = PostPipe()

            def issue(f):
                if f < NF and f not in loads:
                    a = get_tile((id(Wg_s), f), Wg_s.ap()[f])
                    b = get_tile((id(Wu_s), f), Wu_s.ap()[f])
                    loads[f] = (a, b)
            issue(0)
            issue(1)
            for qi, (f0, f1) in enumerate(QUARTERS):
                nfq = f1 - f0
                wdi = qi % 2
                dma(sp, swd[wdi], Wd[wdi][:, 0:nfq, :], Wd_s.ap()[f0:f1].rearrange("n p c -> p n c"), writes=[r_Wd[wdi]])
                for f in range(f0, f1):
                    issue(f + 2)
                    (sa, ha), (sb_, hb) = loads.pop(f)
                    fi = f - f0
                    wg = Wst[sa][:, ha, :].rearrange("p (k m) -> p k m", m=128)
                    wu = Wst[sb_][:, hb, :].rearrange("p (k m) -> p k m", m=128)
                    for sbk in range((ntiles + 3) // 4):
                        c0 = sbk * 512
                        c1 = min(c0 + 512, ntok)
                        n = c1 - c0
                        rxs = r_xnT[sbk * 4:min(sbk * 4 + 4, ntiles)]
                        pi = cnt["ps"] % 2
                        cnt["ps"] += 1
                        pe_multi(pe, [
                            ([lambda k=k, pi=pi, c0=c0, c1=c1, n=n: nc.tensor.matmul(PSh[pi][0][:, 0:n], lhsT=wg[:, k, :], rhs=xnT[:, k, c0:c1],
                                                                                   start=(k == 0), stop=(k == 7)) for k in range(8)],
                             [r_Wst[sa][ha]] + rxs, [r_PS[pi][0]]),
                            ([lambda k=k, pi=pi, c0=c0, c1=c1, n=n: nc.tensor.matmul(PSh[pi][1][:, 0:n], lhsT=wu[:, k, :], rhs=xnT[:, k, c0:c1],
                                                                                   start=(k == 0), stop=(k == 7)) for k in range(8)],
                             [r_Wst[sb_][hb]] + rxs, [r_PS[pi][1]])])
                        si = cnt["sg"] % 2
                        cnt["sg"] += 1
                        op(act, lambda pi=pi, si=si, n=n: nc.scalar.activation(out=sg[si][:, 0:n], in_=PSh[pi][0][:, 0:n], func=AF.Silu),
                           reads=[r_PS[pi][0]], writes=[r_sg[si]])
                        op(dve, lambda pi=pi, si=si, fi=fi, c0=c0, c1=c1, n=n: nc.vector.tensor_tensor(out=actT[:, fi, c0:c1], in0=PSh[pi][1][:, 0:n],
                                                                                     in1=sg[si][:, 0:n], op=ALU.mult),
                           reads=[r_PS[pi][1], r_sg[si]], writes=[r_actT[fi][sbk]])
                last = (qi == len(QUARTERS) - 1)
                if last and after_gateup is not None:
                    after_gateup()
                dbanks = [(BK[4], r_BK[4]), (BK[5], r_BK[5])] if last else [(BK[4], r_BK[4]), (BK[5], r_BK[5]), (BK[6], r_BK[6]), (BK[7], r_BK[7])]
                for t in range(ntiles):
                    bsel = []
                    for dh in range(2):
                        bsel.append(dbanks[cnt["pd"] % len(dbanks)])
                        cnt["pd"] += 1
                    groups = [([lambda fi=fi, t=t, dh=dh, pb=bsel[dh][0]: nc.tensor.matmul(
                        pb, lhsT=actT[:, fi, t * 128:(t + 1) * 128], rhs=Wd[wdi][:, fi, dh * 512:(dh + 1) * 512],
                        start=(fi == 0), stop=(fi == nfq - 1)) for fi in range(nfq)],
                        [r_actT[fi][t // 4] for fi in range(nfq)] + [r_Wd[wdi]], [bsel[dh][1]]) for dh in range(2)]
                    if last:
                        for g in groups:
                            pe_group(pe, g[0], reads=g[1], writes=g[2])
                    else:
                        pe_multi(pe, groups)
                    for dh in range(2):
                        pb, rb = bsel[dh]
                        op(dve, lambda t=t, dh=dh, pb=pb: nc.vector.scalar_tensor_tensor(
                            out=xh[:, t, dh * 512:(dh + 1) * 512], in0=pb, scalar=0.5,
                            in1=xh[:, t, dh * 512:(dh + 1) * 512], op0=ALU.mult, op1=ALU.add),
                           reads=[rb, r_xhh[t][dh]], writes=[r_xhh[t][dh]])
                    if last and post_tile is not None:
                        pp.step(post_tile(t))
                if last:
                    pp.flush()

        def grp_sq(src_ap_f32, ncols, src_res):
            k = cnt["sq"] % 2
            cnt["sq"] += 1
            op(act, lambda: nc.scalar.activation(out=sqb[k][:, 0:ncols], in_=src_ap_f32, func=AF.Square),
               reads=[src_res], writes=[r_sqb[k]])
            return k

        def grp_rest(k, ncols):
            pb, rb, bi = alloc_bank()
            pe_group(pe, [lambda: nc.tensor.matmul(pb[:, 0:ncols], lhsT=onesb[:], rhs=sqb[k][:, 0:ncols], start=True, stop=True)],
                     reads=[r_sqb[k], r_const], writes=[rb])
            op(act, lambda: nc.scalar.activation(out=rst[:, 0:ncols], in_=pb[:, 0:ncols], func=AF.Ln, bias=eps_sb[:], scale=1.0 / 64),
               reads=[rb, r_const], writes=[r_rst])
            release_bank(bi)
            op(act, lambda: nc.scalar.activation(out=rst[:, 0:ncols], in_=rst[:, 0:ncols], func=AF.Exp, scale=-0.5),
               reads=[r_rst], writes=[r_rst])

        def conv_taps(j, ncols):
            w = cw_sb[:, j * 3:(j + 1) * 3]
            op(dve, lambda: nc.vector.tensor_scalar(out=yb[:, 0:ncols], in0=Ub[:, 0:ncols], scalar1=w[:, 0:1], scalar2=None, op0=ALU.mult),
               reads=[r_Ub, r_const], writes=[r_yb])
            op(dve, lambda: nc.vector.scalar_tensor_tensor(out=yb[:, 0:ncols], in0=Ub[:, 1:ncols + 1], scalar=w[:, 1:2], in1=yb[:, 0:ncols],
                                                           op0=ALU.mult, op1=ALU.add), reads=[r_Ub, r_yb, r_const], writes=[r_yb])
            op(dve, lambda: nc.vector.scalar_tensor_tensor(out=yb[:, 0:ncols], in0=Ub[:, 2:ncols + 2], scalar=w[:, 2:3], in1=yb[:, 0:ncols],
                                                           op0=ALU.mult, op1=ALU.add), reads=[r_Ub, r_yb, r_const], writes=[r_yb])
            op(dve, lambda: nc.vector.tensor_tensor(out=yb[:, 0:ncols], in0=yb[:, 0:ncols], in1=Bb[:, 0:ncols], op=ALU.mult),
               reads=[r_yb, r_Bb], writes=[r_yb])
            return grp_sq(yb[:, 0:ncols], ncols, r_yb)

        def conv_out(j, k, ncols, tok_first, own_lo, own_hi):
            grp_rest(k, ncols)
            lo = max(tok_first, own_lo)
            hi = min(tok_first + ncols, own_hi)
            if hi > lo:
                a, b = lo - tok_first, hi - tok_first
                op(dve, lambda: nc.vector.tensor_tensor(out=cT[:, j, lo - own_lo:hi - own_lo], in0=yb[:, a:b], in1=rst[:, a:b], op=ALU.mult),
                   reads=[r_yb, r_rst], writes=[r_cT])

        def conv_finalize(j, ncols, tok_first, own_lo, own_hi):
            k = conv_taps(j, ncols)
            conv_out(j, k, ncols, tok_first, own_lo, own_hi)

        def w_in_phase(unit, blk0, ntiles_all, at_end=None):
            nsb = (ntiles_all + 3) // 4
            own_lo = unit["own_off"]
            own_hi = own_lo + 2048
            order = WIN_ORDER * nsb
            loads = {}
            st = dict(pos=0)

            def issue(i):
                if i < len(order):
                    loads[i] = get_tile(("win", i), Win_s.ap()[order[i]])
            for i in range(4):
                issue(i)
            Wv = Wd[1][:].rearrange("p a (b c) -> p (a b) c", c=512)
            r_Wv = r_Wd[1]
            dma(sp, swd[1], Wv, Wv_s.ap(), writes=[r_Wv])

            def make_items(sbk):
                b0 = blk0 + sbk * 512
                cx = sbk * 512
                ntiles = min(4, ntiles_all - 4 * sbk)
                ntok = ntiles * 128
                t0 = cx // 128
                rx = r_xnT[t0:t0 + ntiles]

                def mm8(pb, rb):
                    slot, hf = loads.pop(st["pos"])
                    wt = Wst[slot][:, hf, :].rearrange("p (k m) -> p k m", m=128)
                    pe_group(pe, [lambda k=k: nc.tensor.matmul(pb[:, 0:ntok], lhsT=wt[:, k, :], rhs=xnT[:, k, cx:cx + ntok],
                                                               start=(k == 0), stop=(k == 7)) for k in range(8)],
                             reads=[r_Wst[slot][hf]] + rx, writes=[rb])
                    issue(st["pos"] + 4)
                    st["pos"] += 1

                def qk_item(n):
                    pb, rb, bi = alloc_bank()
                    mm8(pb, rb)
                    pz = pb[:, 0:ntok]
                    box = {}

                    def s2():
                        box["k"] = grp_sq(pz, ntok, rb)

                    def s3():
                        grp_rest(box["k"], ntok)
                        if n < 4:
                            lo, hi = max(b0, own_lo), min(b0 + ntok, own_hi)
                            if hi > lo:
                                op(dve, lambda: nc.vector.scalar_tensor_tensor(
                                    out=QT[:, n, lo - own_lo:hi - own_lo], in0=pz[:, lo - b0:hi - b0], scalar=gq_sb[:, 0:1],
                                    in1=rst[:, lo - b0:hi - b0], op0=ALU.mult, op1=ALU.mult),
                                   reads=[rb, r_rst, r_const], writes=[r_QT])
                        else:
                            op(dve, lambda: nc.vector.scalar_tensor_tensor(
                                out=KTt[:, n - 4, b0:b0 + ntok], in0=pz, scalar=gk_sb[:, 0:1], in1=rst[:, 0:ntok],
                                op0=ALU.mult, op1=ALU.mult), reads=[rb, r_rst, r_const], writes=[r_KT])
                        release_bank(bi)
                    return s2, s3

                def v_item(t):
                    pb, rb, bi = alloc_bank()
                    tt = t0 + t
                    pe_group(pe, [lambda k=k: nc.tensor.matmul(pb, lhsT=xnT[:, k, tt * 128:(tt + 1) * 128], rhs=Wv[:, k, :],
                                                               start=(k == 0), stop=(k == 7)) for k in range(8)],
                             reads=[r_xnT[tt], r_Wv], writes=[rb])
                    vt = b0 // 128 + t

                    def s2():
                        copy_on(evac_eng(), V[:, vt, :].rearrange("p (h e) -> p h e", e=65)[:, :, 0:64],
                                pb.rearrange("p (h e) -> p h e", e=64), [rb], [r_V])
                        release_bank(bi)
                    return s2, None

                def conv_item(j):
                    bks = [alloc_bank() for _ in range(3)]
                    for (pb, rb, _bi) in bks:
                        mm8(pb, rb)
                    (pB, rB, iB), (pC, rC, iC), (pX, rX, iX) = bks
                    box = {}

                    def s2():
                        op(dve, lambda: nc.vector.tensor_copy(out=Ub[:, 0:2], in_=Ucar[:, j, :]), reads=[r_car], writes=[r_Ub])
                        op(dve, lambda: nc.vector.tensor_copy(out=Bb[:, 0:1], in_=Bcar[:, j, :]), reads=[r_car], writes=[r_Bb])
                        op(act, lambda: nc.scalar.copy(out=Csb[:, 0:ntok], in_=pC[:, 0:ntok]), reads=[rC], writes=[r_Csb])
                        op(dve, lambda: nc.vector.tensor_tensor(out=Ub[:, 2:ntok + 2], in0=pX[:, 0:ntok], in1=Csb[:, 0:ntok], op=ALU.mult),
                           reads=[rX, r_Csb], writes=[r_Ub])
                        op(act, lambda: nc.scalar.copy(out=Bb[:, 1:ntok + 1], in_=pB[:, 0:ntok]), reads=[rB], writes=[r_Bb])
                        op(dve, lambda: nc.vector.tensor_copy(out=Ucar[:, j, :], in_=Ub[:, ntok:ntok + 2]), reads=[r_Ub], writes=[r_car])
                        op(dve, lambda: nc.vector.tensor_copy(out=Bcar[:, j, :], in_=Bb[:, ntok:ntok + 1]), reads=[r_Bb], writes=[r_car])
                        for _i in (iB, iC, iX):
                            release_bank(_i)
                        box["k"] = conv_taps(j, ntok)

                    def s3():
                        conv_out(j, box["k"], ntok, b0 - 1, own_lo, own_hi)
                    return s2, s3


                items = [(qk_item, n) for n in range(8)]
                for j in range(4):
                    items.append((conv_item, j))
                    if j < ntiles:
                        items.append((v_item, j))
                return items

            items = []
            for sbk in range(nsb):
                items += make_items(sbk)
            hist = []
            for ii, (fn, a) in enumerate(items + [(None, None), (None, None)]):
                if fn is not None:
                    hist.append(fn(a))
                else:
                    hist.append((None, None))
                if ii == len(items) - 1 and at_end is not None:
                    at_end()
                if len(hist) >= 3 and hist[-3][1] is not None:
                    hist[-3][1]()
                if len(hist) >= 2 and hist[-2][0] is not None:
                    hist[-2][0]()

        def conv_flush(unit):
            own_lo = unit["own_off"]
            own_hi = own_lo + 2048
            last = unit["n_ext"] - 1
            if not (own_lo <= last < own_hi):
                return
            for j in range(4):
                op(dve, lambda j=j: nc.vector.tensor_copy(out=Ub[:, 0:2], in_=Ucar[:, j, :]), reads=[r_car], writes=[r_Ub])
                op(dve, lambda: nc.vector.memset(Ub[:, 2:4], 0.0), writes=[r_Ub])
                op(dve, lambda j=j: nc.vector.tensor_copy(out=Bb[:, 0:1], in_=Bcar[:, j, :]), reads=[r_car], writes=[r_Bb])
                op(dve, lambda: nc.vector.memset(Bb[:, 1:2], 0.0), writes=[r_Bb])
                conv_finalize(j, 2, last, own_lo, own_hi)

        def attention_block(unit, q0):
            kind = unit["kind"]
            seq = [(tl, h) for tl in range(NTB) for h in range(8)]

            def scores(tl, h):
                plan = plans[kind][q0 + tl]
                combos, runs, y0 = plan["combos"], plan["runs"], plan["y0"]
                q = q0 + tl
                ncol = len(combos) * 128
                pi = cnt["ps3"] % 3
                cnt["ps3"] += 1
                hp, hf = h // 2, h % 2
                ps = SLAB[pi]
                pe_group(pe, [lambda ci=ci, kp=kp: nc.tensor.matmul(
                    ps[:, ci * 128:(ci + 1) * 128], lhsT=KTt[hf * 64:(hf + 1) * 64, hp, kp * 128:(kp + 1) * 128],
                    rhs=QT[hf * 64:(hf + 1) * 64, hp, q * 128:(q + 1) * 128], start=True, stop=True)
                    for ci, kp in enumerate(combos)],
                    reads=[r_KT, r_QT], writes=r_SLAB[pi])
                off = (y0 - 2) * 64
                op(dve, lambda: nc.vector.tensor_tensor(out=Sb[pi][:, 0:ncol], in0=ps[:, 0:ncol], in1=TT3[:, h, off:off + ncol], op=ALU.add),
                   reads=r_SLAB[pi] + [r_const], writes=[r_Sb[pi]])
                for (c0, c1, mc) in runs:
                    if mc == 0:
                        op(act, lambda c0=c0, c1=c1: nc.scalar.activation(out=PT[pi][:, c0:c1], in_=Sb[pi][:, c0:c1], func=AF.Exp),
                           reads=[r_Sb[pi]], writes=[r_PT[pi]])
                    elif mc == 3:
                        op(dve, lambda c0=c0, c1=c1: nc.vector.memset(PT[pi][:, c0:c1], 0.0), writes=[r_PT[pi]])
                    else:
                        op(act, lambda c0=c0, c1=c1, mc=mc: nc.scalar.activation(out=PT[pi][:, c0:c1], in_=Sb[pi][:, c0:c1], func=AF.Exp,
                                                                                 bias=maskt[:, mc:mc + 1]),
                           reads=[r_Sb[pi], r_const], writes=[r_PT[pi]])
                return pi

            def pv(tl, h, pi):
                combos = plans[kind][q0 + tl]["combos"]
                nco = len(combos)
                o = BK[6][:, h * 65:h * 65 + 65] if h < 7 else BK[7][:, 0:65]
                first = (h == 0 or h == 7)
                rb = r_BK[6] if h < 7 else r_BK[7]
                pe_group(pe, [lambda ci=ci, kp=kp: nc.tensor.matmul(
                    o, lhsT=PT[pi][:, ci * 128:(ci + 1) * 128], rhs=V[:, kp, h * 65:(h + 1) * 65],
                    start=(ci == 0), stop=(ci == nco - 1)) for ci, kp in enumerate(combos)],
                    reads=[r_PT[pi], r_V], writes=[rb] if first else [])
                if not first:
                    rb.w = pe.cur()

            def tail_s1(tl):
                for (bk, h0, nh, c0) in ((6, 0, 7, 16), (7, 7, 1, 23)):
                    ov = BK[bk][:, 0:nh * 65].rearrange("p (h e) -> p h e", e=65)
                    op(dve, lambda ov=ov, c0=c0, nh=nh: nc.vector.reciprocal(out=st8[:, c0:c0 + nh].unsqueeze(2), in_=ov[:, :, 64:65]),
                       reads=[r_BK[bk]], writes=[r_st8])
                    op(dve, lambda ov=ov, c0=c0, nh=nh, h0=h0: nc.vector.tensor_tensor(
                        out=atok[:, h0 * 64:(h0 + nh) * 64].rearrange("p (h e) -> p h e", e=64), in0=ov[:, :, 0:64],
                        in1=st8[:, c0:c0 + nh].unsqueeze(2).broadcast_to([128, nh, 64]), op=ALU.mult),
                       reads=[r_BK[bk], r_st8], writes=[r_atok])
                op(act, lambda: nc.scalar.activation(out=junk[:, 0:512], in_=atok, func=AF.Square), reads=[r_atok], writes=[r_junk])

            def tail_s2(tl):
                op(dve, lambda: nc.vector.tensor_reduce(out=st8[:, 24:32], in_=junk[:, 0:512].rearrange("p (h e) -> p h e", e=64),
                                                        axis=AX.X, op=ALU.add), reads=[r_junk], writes=[r_st8b])
                rstd_small(st8[:, 24:32], st8[:, 32:40], 1.0 / 64, r_st8b)

            def tail_s3(tl):
                op(dve, lambda: nc.vector.tensor_tensor(out=abf.rearrange("p (h e) -> p h e", e=64),
                                                        in0=atok.rearrange("p (h e) -> p h e", e=64),
                                                        in1=st8[:, 32:40].unsqueeze(2).broadcast_to([128, 8, 64]), op=ALU.mult),
                   reads=[r_atok, r_st8b], writes=[r_abf])

            def tail_pe(tl):
                tpv = TP[1][:, 512:1024].rearrange("p (k m) -> p k m", m=128)
                pe_group(pe, [lambda k=k: nc.tensor.transpose(tpv[:, k, :], abf[:, k * 128:(k + 1) * 128], ident[:]) for k in range(4)],
                         reads=[r_abf, r_const], writes=[r_BK[7]])
                copy_on(evac_eng(), xnT[:, 0:4, tl * 128:(tl + 1) * 128], tpv, [r_BK[7]], [r_xnT[tl]])

            LOOK = 2
            slabs = {}
            todo = []
            for i in range(min(LOOK, len(seq))):
                slabs[i] = scores(*seq[i])
            for i, (tl, h) in enumerate(seq):
                if i + LOOK < len(seq):
                    slabs[i + LOOK] = scores(*seq[i + LOOK])
                pv(tl, h, slabs.pop(i))
                while todo and todo[0][0] <= i:
                    _, fn, a = todo.pop(0)
                    fn(a)
                if h == 7:
                    tail_s1(tl)
                    todo += [(i + 1, tail_s2, tl), (i + 2, tail_s3, tl), (i + 4, tail_pe, tl)]
            for _, fn, a in todo:
                fn(a)

        for unit in units:
            n_ext = unit["n_ext"]
            own_lo = unit["own_off"]
            own_hi = own_lo + 2048
            fence()
            op(dve, lambda: nc.vector.memset(Ucar[:], 0.0), writes=[r_car])
            op(dve, lambda: nc.vector.memset(Bcar[:], 0.0), writes=[r_car])
            blocks_a = list(range(0, n_ext, T))
            for b0 in blocks_a:
                ntiles = min(NTB, (n_ext - b0) // 128)
                for t in range(ntiles):
                    r0 = unit["in0"] + b0 + t * 128
                    dma(sp, sx[t], xh[:, t, :], xin[r0:r0 + 128, :], writes=r_xhh[t])
                norm_tiles(ntiles)

                def post_a(t, b0=b0):
                    A, B, C = norm_stages(t)

                    def A2():
                        A()
                        tok = b0 + t * 128
                        if own_lo <= tok < own_hi:
                            ot = (tok - own_lo) // 128
                            dma(pool, sxs[t], h_s.ap()[ot * 128:(ot + 1) * 128, :], xh[:, t, :], reads=r_xhh[t], writes=[r_hs[ot]])
                    return A2, B, C
                ffn(W1g_s, W1u_s, W1d_s, ntiles, post_a, after_gateup=prefetch_win)
                w_in_phase(unit, b0, ntiles,
                           at_end=None if b0 == blocks_a[-1] else (lambda: prefetch_ffn(W1g_s, W1u_s)))
            conv_flush(unit)
            fence()
            for ob in range(0, 2048, T):
                for t in range(NTB):
                    ot = ob // 128 + t
                    dma(sp, sx[t], xh[:, t, :], h_s.ap()[ot * 128:(ot + 1) * 128, :], reads=[r_hs[ot]], writes=r_xhh[t])
                dma(sp, swd[0], Wd[0][:, 0:4, :], Wout_s.ap()[:, 0:4, :], writes=[r_Wd[0]])
                dma(sp, swd[1], Wd[1][:, 0:4, :], Wout_s.ap()[:, 4:8, :], writes=[r_Wd[1]])
                prefetch_ffn(W2g_s, W2u_s)
                attention_block(unit, ob // 128)
                wpp = PostPipe()
                for t in range(NTB):
                    ot = ob // 128 + t
                    for dh in range(2):
                        di = cnt["pd"] % 2
                        cnt["pd"] += 1
                        fns = []
                        for k in range(8):
                            if k < 4:
                                fns.append(lambda k=k, t=t, dh=dh, di=di: nc.tensor.matmul(
                                    PD[di], lhsT=xnT[:, k, t * 128:(t + 1) * 128], rhs=Wd[0][:, k, dh * 512:(dh + 1) * 512],
                                    start=(k == 0), stop=False))
                            else:
                                fns.append(lambda k=k, ot=ot, dh=dh, di=di: nc.tensor.matmul(
                                    PD[di], lhsT=cT[:, k - 4, ot * 128:(ot + 1) * 128], rhs=Wd[1][:, k - 4, dh * 512:(dh + 1) * 512],
                                    start=False, stop=(k == 7)))
                        pe_group(pe, fns, reads=[r_xnT[t], r_cT, r_Wd[0], r_Wd[1]], writes=[r_PD[di]])
                        op(dve, lambda t=t, dh=dh, di=di: nc.vector.tensor_tensor(
                            out=xh[:, t, dh * 512:(dh + 1) * 512], in0=PD[di], in1=xh[:, t, dh * 512:(dh + 1) * 512], op=ALU.add),
                           reads=[r_PD[di], r_xhh[t][dh]], writes=[r_xhh[t][dh]])
                    wpp.step(norm_stages(t))
                wpp.flush()

                def post_b(t, ob=ob):
                    def A():
                        op(act, lambda: nc.scalar.activation(out=junk[:], in_=xh[:, t, :], func=AF.Square, accum_out=st8[:, 40 + t:41 + t]),
                           reads=r_xhh[t], writes=[r_junk, r_fs[t]])
                        rstd_small(st8[:, 40 + t:41 + t], st8[:, 48 + t:49 + t], 1.0 / D, r_fs[t])

                    def B():
                        op(dve, lambda: nc.vector.scalar_tensor_tensor(out=xh[:, t, :], in0=xh[:, t, :], scalar=st8[:, 48 + t:49 + t],
                                                                       in1=gfin_sb[:], op0=ALU.mult, op1=ALU.mult),
                           reads=r_xhh[t] + [r_fs[t], r_const], writes=r_xhh[t])
                        r0 = unit["y0"] + ob + t * 128
                        dma(pool, sxs[t], yout[r0:r0 + 128, :], xh[:, t, :], reads=r_xhh[t])
                    return A, B, None
                nxt_unit = (ob + T >= 2048) and (unit is not units[-1])
                ffn(W2g_s, W2u_s, W2d_s, NTB, post_b,
                    after_gateup=(lambda: prefetch_ffn(W1g_s, W1u_s)) if nxt_unit else None)
        for e in (sp, pool):
            for s in streams:
                e.wait(s.cur())
    return nc, dict(NTOK_IN=NTOK_IN, NTOK_OUT=NTOK_OUT, NM=NM, dyn_cols=dyn_cols, units=units)


def _tile8(g):
    return np.ascontiguousarray(np.asarray(g, np.float32).reshape(8, 128).T)


def common_inputs(p, dyn_cols, variant):
    onesblk = np.zeros((128, 128), np.float32)
    onesblk[:64, :64] = 1.0
    onesblk[64:, 64:] = 1.0
    cols = np.arange(64)
    cs = np.clip(cols - 8, 0, 48)
    colmask = np.where((cols[:, None] >= cs[None, :]) & (cols[:, None] < cs[None, :] + 16), 0.0, NEG).astype(np.float32)
    colmask = np.concatenate([colmask, colmask], 0)
    rpb = np.asarray(p["rel_pos_bias"], np.float32)[0]
    d = dict(
        g1=_tile8(p["ffn1_norm"][0]), g2=_tile8(p["ffn2_norm"][0]), gm=_tile8(p["mix_norm"][0]),
        go=_tile8(np.concatenate([np.asarray(p["attn_out_norm"][0]), np.asarray(p["conv_out_norm"][0])])),
        gq=np.ascontiguousarray(np.tile(np.asarray(p["q_norm"][0], np.float32), 2)[:, None]),
        gk=np.ascontiguousarray(np.tile(np.asarray(p["k_norm"][0], np.float32), 2)[:, None]),
        convw=np.ascontiguousarray(np.asarray(p["conv_w"][0], np.float32).reshape(3, 4, 128).transpose(2, 1, 0).reshape(128, 12)),
        gfin=np.ascontiguousarray(np.broadcast_to(np.asarray(p["final_norm"][0], np.float32)[None, :], (128, D))),
        w1g=np.asarray(p["ffn1_w_gate"][0], np.float32), w1u=np.asarray(p["ffn1_w_up"][0], np.float32),
        w1d=np.asarray(p["ffn1_w_down"][0], np.float32), win=np.asarray(p["w_in"][0], np.float32),
        wout=np.asarray(p["w_out"][0], np.float32), w2g=np.asarray(p["ffn2_w_gate"][0], np.float32),
        w2u=np.asarray(p["ffn2_w_up"][0], np.float32), w2d=np.asarray(p["ffn2_w_down"][0], np.float32),
        rpb_rev=np.ascontiguousarray(rpb[:, :, ::-1].reshape(120, 31)),
        ident=np.eye(128, dtype=np.float32), onesblk=onesblk, colmask=colmask,
        maskt=mask_table(dyn_cols, variant),
        mask01=(mask_table(dyn_cols, variant) == 0).astype(np.float32),
    )
    return d


def sample_ext(xs_seq, half):
    ext = np.zeros((2560, D), np.float32)
    if half == 0:
        ext[256:2560] = xs_seq[0:2304]
    else:
        ext[0:2304] = xs_seq[2048 - 256:4096]
    return ext


_CACHE = {}


def kernel(**inputs):
    x_prompt = np.asarray(inputs["x_prompt"], np.float32)
    x_sample = np.asarray(inputs["x_sample"], np.float32)
    if "prog" not in _CACHE:
        _CACHE["prog"] = build_program(4, True)
    nc, meta = _CACHE["prog"]
    in_maps = []
    for c in range(8):
        half = c % 2
        d = common_inputs(inputs, meta["dyn_cols"], half)
        xin = np.concatenate([x_prompt[4 * c:4 * c + 4].reshape(4 * 2048, D), sample_ext(x_sample[c // 2], half)], 0)
        d["xin"] = np.ascontiguousarray(xin)
        in_maps.append(d)
    res = run_bass_kernel_spmd(nc, in_maps, core_ids=list(range(8)))
    y_prompt = np.empty((32, 2048, D), np.float32)
    y_sample = np.empty((4, 4096, D), np.float32)
    for c in range(8):
        y = res.results[c]["yout"]
        y_prompt[4 * c:4 * c + 4] = y[0:8192].reshape(4, 2048, D)
        half = c % 2
        y_sample[c // 2, half * 2048:(half + 1) * 2048] = y[8192:10240]
    return (y_prompt, y_sample)
```

```python
import numpy as np
from contextlib import ExitStack
import concourse.bass as bass
import concourse.mybir as mybir
from concourse.bass_utils import run_bass_kernel_spmd

F32 = mybir.dt.float32
BF16 = mybir.dt.bfloat16
AF = mybir.ActivationFunctionType
ALU = mybir.AluOpType
AX = mybir.AxisListType

D = 1024
FF = 2816
NF = 22
T = 1024
NTB = T // 128
NEG = -30000.0
EPS = 1e-6
QUARTERS = [(0, 3), (3, 6), (6, 10), (10, 14), (14, 18), (18, 22)]
NFQ = 4
NYY = 14


def _unit_struct(kind):
    if kind == "prompt":
        R_ext, own0 = 32, 0
        ws = [lambda e: min(max(e - 4, 0), 24)] * 2
        exist = [lambda kr: 0 <= kr < 32] * 2
    else:
        R_ext, own0 = 40, 4
        ws = [lambda e: max(e - 4, 4), lambda e: min(e - 4, 28)]
        exist = [lambda kr: 4 <= kr < 40, lambda kr: 0 <= kr < 36]
    return R_ext, own0, ws, exist


def build_attn_plan():
    static_cols = {(True, True): 0, (True, False): 1, (False, True): 2, (False, False): 3}
    dyn_cols = []
    plans = {}
    for kind in ("prompt", "sample"):
        R_ext, own0, ws, exist = _unit_struct(kind)
        qps = []
        for q in range(16):
            e0 = own0 + 2 * q
            rows = set()
            for v in range(2):
                for j in range(2):
                    s = ws[v](e0 + j)
                    rows.update(range(s, s + 8))
            kp_lo, kp_hi = min(rows) // 2, max(rows) // 2
            combos = list(range(kp_hi, kp_lo - 1, -1))
            runs = []
            for ci, kp in enumerate(combos):
                for j in range(2):
                    pats = []
                    for v in range(2):
                        s = ws[v](e0 + j)
                        pat = tuple((s <= 2 * kp + i < s + 8) and exist[v](2 * kp + i) for i in range(2))
                        pats.append(pat)
                    if pats[0] == pats[1]:
                        col = static_cols[pats[0]]
                    else:
                        dyn_cols.append((pats[0], pats[1]))
                        col = 4 + len(dyn_cols) - 1
                    c0 = ci * 128 + j * 64
                    if runs and runs[-1][2] == col and col < 4 and runs[-1][1] == c0:
                        runs[-1] = (runs[-1][0], c0 + 64, col)
                    else:
                        runs.append((c0, c0 + 64, col))
            qp_ext = e0 // 2
            dmax = 2 * (combos[0] - qp_ext)
            y0 = 8 - dmax
            assert 2 <= y0 and y0 + 2 * len(combos) <= 16, (kind, q, y0, combos)
            qps.append(dict(qp_ext=qp_ext, combos=combos, runs=runs, y0=y0))
        plans[kind] = qps
    return plans, dyn_cols


def mask_table(dyn_cols, variant):
    nm = 4 + len(dyn_cols)
    m = np.zeros((128, nm), np.float32)
    m[64:, 1] = NEG
    m[:64, 2] = NEG
    m[:, 3] = NEG
    for k, pats in enumerate(dyn_cols):
        p = pats[variant]
        if not p[0]:
            m[:64, 4 + k] = NEG
        if not p[1]:
            m[64:, 4 + k] = NEG
    return m


class Res:
    __slots__ = ("w", "r", "name")

    def __init__(self, name=""):
        self.w = None
        self.r = {}
        self.name = name


class EngQ:
    def __init__(self, nc, es, eng, name):
        self.eng = eng
        self.sem = es.enter_context(nc.semaphore("s_" + name))
        self.count = 0
        self.waited = {}
        self.name = name

    def wait(self, tok):
        if tok is None:
            return
        sem, val = tok
        k = id(sem)
        if self.waited.get(k, 0) >= val:
            return
        self.eng.wait_ge(sem, val)
        self.waited[k] = val

    def signal(self, inst):
        self.count += 1
        inst.then_inc(self.sem, 1)
        return (self.sem, self.count)

    def cur(self):
        return (self.sem, self.count) if self.count else None


class Stream:
    def __init__(self, nc, es, name):
        self.sem = es.enter_context(nc.semaphore("d_" + name))
        self.n = 0

    def cur(self):
        return (self.sem, 16 * self.n) if self.n else None


def _deps(q, reads, writes):
    toks = []
    own = q.sem
    for r in reads:
        if r.w is not None and not (r.w[0] is own and q.name == "pe"):
            toks.append(r.w)
    for w in writes:
        if w.w is not None and w.w[0] is not own:
            toks.append(w.w)
        for t in w.r.values():
            if t[0] is not own:
                toks.append(t)
    return toks


def _commit(tok, reads, writes):
    for r in reads:
        k = id(tok[0])
        old = r.r.get(k)
        if old is None or old[1] < tok[1]:
            r.r[k] = tok
    for w in writes:
        w.w = tok
        w.r = {}


def op(q, fn, reads=(), writes=()):
    for t in _deps(q, reads, writes):
        q.wait(t)
    inst = fn()
    tok = q.signal(inst)
    _commit(tok, reads, writes)
    return tok


def pe_group(q, fns, reads=(), writes=()):
    for t in _deps(q, reads, writes):
        q.wait(t)
    inst = None
    for fn in fns:
        inst = fn()
    tok = q.signal(inst)
    _commit(tok, reads, writes)
    return tok


def pe_multi(q, groups):
    for fns, reads, writes in groups:
        for t in _deps(q, reads, writes):
            q.wait(t)
    for fns, reads, writes in groups:
        inst = None
        for fn in fns:
            inst = fn()
        tok = q.signal(inst)
        _commit(tok, reads, writes)


def dma(q, stream, out, in_, reads=(), writes=(), **kw):
    q.wait(stream.cur())
    for t in _deps(q, reads, writes):
        q.wait(t)
    inst = q.eng.dma_start(out=out, in_=in_, **kw)
    stream.n += 1
    inst.then_inc(stream.sem, 16)
    tok = (stream.sem, 16 * stream.n)
    _commit(tok, reads, writes)
    return tok


def build_program(n_prompt=4, with_sample=True):
    plans, dyn_cols = build_attn_plan()
    NM = 4 + len(dyn_cols)
    units = []
    tok_in = 0
    for u in range(n_prompt):
        units.append(dict(kind="prompt", in0=tok_in, n_ext=2048, own_off=0, y0=u * 2048))
        tok_in += 2048
    if with_sample:
        units.append(dict(kind="sample", in0=tok_in, n_ext=2560, own_off=256, y0=n_prompt * 2048))
        tok_in += 2560
    NTOK_IN = tok_in
    NTOK_OUT = len(units) * 2048

    nc = bass.Bass("TRN2", target_bir_lowering=False)

    def din(name, shape, dt=F32):
        return nc.dram_tensor(name, list(shape), dt, kind="ExternalInput")

    def dscr(name, shape, dt):
        return nc.dram_tensor(name, list(shape), dt, kind="Internal")

    xin = din("xin", [NTOK_IN, D]).ap()
    yout = nc.dram_tensor("yout", [NTOK_OUT, D], F32, kind="ExternalOutput").ap()
    g1_d = din("g1", [128, 8]).ap()
    g2_d = din("g2", [128, 8]).ap()
    gm_d = din("gm", [128, 8]).ap()
    go_d = din("go", [128, 8]).ap()
    gq_d = din("gq", [128, 1]).ap()
    gk_d = din("gk", [128, 1]).ap()
    cw_d = din("convw", [128, 12]).ap()
    gfin_d = din("gfin", [128, D]).ap()
    w1g_d = din("w1g", [D, FF]).ap()
    w1u_d = din("w1u", [D, FF]).ap()
    w1d_d = din("w1d", [FF, D]).ap()
    win_d = din("win", [D, 3072]).ap()
    wout_d = din("wout", [D, D]).ap()
    w2g_d = din("w2g", [D, FF]).ap()
    w2u_d = din("w2u", [D, FF]).ap()
    w2d_d = din("w2d", [FF, D]).ap()
    rpbr_d = din("rpb_rev", [120, 31]).ap()
    ident_d = din("ident", [128, 128]).ap()
    ones_d = din("onesblk", [128, 128]).ap()
    colmask_d = din("colmask", [128, 64]).ap()
    maskt_d = din("maskt", [128, NM]).ap()
    mask01_d = din("mask01", [128, NM]).ap()

    W1g_s = dscr("W1g_s", [NF, 128, 1024], BF16)
    W1u_s = dscr("W1u_s", [NF, 128, 1024], BF16)
    W2g_s = dscr("W2g_s", [NF, 128, 1024], BF16)
    W2u_s = dscr("W2u_s", [NF, 128, 1024], BF16)
    W1d_s = dscr("W1d_s", [NF, 128, 1024], BF16)
    W2d_s = dscr("W2d_s", [NF, 128, 1024], BF16)
    Win_s = dscr("Win_s", [24, 128, 1024], BF16)
    Wv_s = dscr("Wv_s", [128, 8, 512], BF16)
    Wout_s = dscr("Wout_s", [128, 8, 1024], BF16)
    h_s = dscr("h_s", [2048, D], F32)
    rpbp = dscr("rpbp", [120, 160], F32)
    rpbsk = dscr("rpbsk", [120, 64, 64], F32)

    with ExitStack() as es:
        pe = EngQ(nc, es, nc.tensor, "pe")
        act = EngQ(nc, es, nc.scalar, "act")
        dve = EngQ(nc, es, nc.vector, "dve")
        pool = EngQ(nc, es, nc.gpsimd, "pool")
        sp = EngQ(nc, es, nc.sync, "sp")
        engs = [pe, act, dve, pool, sp]
        sx = [Stream(nc, es, f"x{i}") for i in range(NTB)]
        sxs = [Stream(nc, es, f"xs{i}") for i in range(NTB)]
        sw = [[Stream(nc, es, f"w{i}_{j}") for j in range(2)] for i in range(3)]
        swd = [Stream(nc, es, f"wd{i}") for i in range(2)]
        st_su = Stream(nc, es, "su")
        su_pool = [Stream(nc, es, f"su{i}") for i in range(32)]
        su_i = [0]

        def su_next():
            st = su_pool[su_i[0] % len(su_pool)]
            su_i[0] += 1
            return st
        s_wrow = [Stream(nc, es, f"wrow{i}") for i in range(2)]
        s_stage = [Stream(nc, es, f"stage{i}") for i in range(2)]
        streams = sx + sxs + sw[0] + sw[1] + sw[2] + swd + [st_su] + su_pool + s_wrow + s_stage

        def barrier():
            toks = [e.cur() for e in engs if e is not sp] + [s.cur() for s in streams]
            for e in engs:
                for t in toks:
                    if t is not None and t[0] is not e.sem:
                        e.wait(t)

        def sb(name, shape, dt=F32):
            return es.enter_context(nc.sbuf_tensor("sb_" + name, list(shape), dt))

        ident = sb("ident", [128, 128], BF16)
        onesb = sb("onesb", [128, 128], BF16)
        g_sb = sb("g_sb", [128, 4, 8])
        gq_sb = sb("gq_sb", [128, 1])
        gk_sb = sb("gk_sb", [128, 1])
        cw_sb = sb("cw_sb", [128, 12])
        gfin_sb = sb("gfin_sb", [128, D])
        eps_sb = sb("eps_sb", [128, 1])
        maskt = sb("maskt", [128, NM])
        mask01 = sb("mask01", [128, NM])
        TT3 = sb("TT3", [128, 8, NYY * 64], BF16)
        r_const = Res("const")

        with ExitStack() as es2:
            def sb2(name, shape, dt=F32):
                return es2.enter_context(nc.sbuf_tensor("s2_" + name, list(shape), dt))
            tmpf = sb2("tmpf", [128, 256])
            zt = sb2("zt", [128, 160])
            colmask = sb2("colmask", [128, 64])
            TT3f = sb2("TT3f", [128, 8, NYY * 64])
            wrow = [sb2(f"wrow{i}", [128, 3072]) for i in range(2)]
            stage = [sb2(f"stage{i}", [128, 24 * 1024], BF16) for i in range(2)]
            r_wrow = [Res(), Res()]
            r_stage = [Res(), Res()]
            r_tmp = Res()

            r_params = []
            for dst, src in ((g_sb[:, 0, :], g1_d), (g_sb[:, 1, :], g2_d), (g_sb[:, 2, :], gm_d),
                             (g_sb[:, 3, :], go_d), (gq_sb[:], gq_d), (gk_sb[:], gk_d), (cw_sb[:], cw_d),
                             (gfin_sb[:], gfin_d), (maskt[:], maskt_d), (mask01[:], mask01_d), (colmask[:], colmask_d),
                             (tmpf[:, 0:128], ident_d), (tmpf[:, 128:256], ones_d)):
                rp_ = Res()
                r_params.append(rp_)
                dma(sp, su_next(), dst, src, writes=[rp_])
            op(dve, lambda: nc.vector.tensor_copy(out=ident[:], in_=tmpf[:, 0:128]), reads=r_params, writes=[r_tmp])
            op(dve, lambda: nc.vector.tensor_copy(out=onesb[:], in_=tmpf[:, 128:256]), reads=r_params, writes=[r_tmp])
            op(dve, lambda: nc.vector.tensor_scalar(out=gq_sb[:], in0=gq_sb[:], scalar1=0.125, scalar2=None, op0=ALU.mult),
               reads=r_params, writes=[r_tmp])
            op(dve, lambda: nc.vector.memset(eps_sb[:], EPS), writes=[r_tmp])
            op(dve, lambda: nc.vector.memset(zt[:], 0.0), writes=[r_tmp])
            r_rp = Res()
            dma(sp, st_su, rpbp.ap(), zt[0:120, :], reads=[r_tmp], writes=[r_rp])
            dma(sp, st_su, rpbp.ap()[:, 64:95], rpbr_d, reads=[], writes=[r_rp])
            r_sk = []
            for r3 in range(0, 120, 40):
                srcsk = bass.AP(tensor=rpbp, offset=r3 * 160 + 79, ap=[[160, 40], [-1, 64], [1, 64]])
                rs_ = Res()
                r_sk.append(rs_)
                dma(sp, su_next(), rpbsk.ap()[r3:r3 + 40], srcsk, reads=[r_rp], writes=[rs_])
            r_tt = {}
            for h in range(8):
                for i in range(2):
                    src = bass.AP(tensor=rpbsk, offset=(h * 15 + i + 13) * 4096,
                                  ap=[[64, 64], [-4096, NYY], [1, 64]])
                    dst = TT3f[i * 64:(i + 1) * 64, h, :].rearrange("p (y c) -> p y c", c=64)
                    r_tt[(h, i)] = Res()
                    dma(sp, su_next(), dst, src, reads=r_sk, writes=[r_tt[(h, i)]])
            for h in range(8):
                op(dve, lambda h=h: nc.vector.tensor_tensor(
                    out=TT3[:, h, :].rearrange("p (y c) -> p y c", c=64),
                    in0=TT3f[:, h, :].rearrange("p (y c) -> p y c", c=64),
                    in1=colmask[:].unsqueeze(1).broadcast_to([128, NYY, 64]), op=ALU.add),
                   reads=[r_tt[(h, 0)], r_tt[(h, 1)]] + r_params, writes=[r_tmp])

            cast_i = [0]

            def cast(out_ap, in_ap, gain_ap, reads, writes):
                i = cast_i[0]
                cast_i[0] += 1
                if i % 2 == 0:
                    if gain_ap is None:
                        op(act, lambda: nc.scalar.copy(out=out_ap, in_=in_ap), reads=reads, writes=writes)
                    else:
                        op(act, lambda: nc.scalar.activation(out=out_ap, in_=in_ap, func=AF.Identity, scale=gain_ap),
                           reads=reads + r_params, writes=writes)
                else:
                    if gain_ap is None:
                        op(dve, lambda: nc.vector.tensor_copy(out=out_ap, in_=in_ap), reads=reads, writes=writes)
                    else:
                        op(dve, lambda: nc.vector.tensor_scalar(out=out_ap, in0=in_ap, scalar1=gain_ap, scalar2=None,
                                                                op0=ALU.mult), reads=reads + r_params, writes=writes)

            mat_i = [0]
            row_i = [0]

            def prep_colmajor(src, ncols, gidx, dst_s, with_v=False):
                si = mat_i[0] % 2
                mat_i[0] += 1
                nt = ncols // 128
                stg = stage[si][:, 0:nt * 1024].rearrange("p (n k m) -> p n k m", k=8, m=128)
                for k in range(8):
                    ri = row_i[0] % 2
                    row_i[0] += 1
                    dma(sp, s_wrow[ri], wrow[ri][:, 0:ncols], src[k * 128:(k + 1) * 128, :], writes=[r_wrow[ri]])
                    cast(stg[:, :, k, :], wrow[ri][:, 0:ncols].rearrange("p (n m) -> p n m", m=128),
                         g_sb[:, gidx, k:k + 1], [r_wrow[ri]], [r_stage[si]])
                dma(pool, s_stage[si], dst_s.ap().rearrange("n p c -> p n c"), stage[si][:, 0:nt * 1024].rearrange("p (n c) -> p n c", c=1024),
                    reads=[r_stage[si]])
                if with_v:
                    for n in range(4):
                        dma(pool, s_stage[si], Wv_s.ap()[:, :, n * 128:(n + 1) * 128], stg[:, 8 + n, :, :], reads=[r_stage[si]])

            def prep_rowmajor(src, nrt, gidx, dst_ap):
                si = mat_i[0] % 2
                mat_i[0] += 1
                for f in range(nrt):
                    ri = row_i[0] % 2
                    row_i[0] += 1
                    dma(sp, s_wrow[ri], wrow[ri][:, 0:1024], src[f * 128:(f + 1) * 128, :], writes=[r_wrow[ri]])
                    cast(stage[si][:, f * 1024:(f + 1) * 1024], wrow[ri][:, 0:1024],
                         None if gidx is None else g_sb[:, gidx, f:f + 1], [r_wrow[ri]], [r_stage[si]])
                dma(pool, s_stage[si], dst_ap, stage[si][:, 0:nrt * 1024].rearrange("p (n c) -> p n c", c=1024),
                    reads=[r_stage[si]])

            prep_colmajor(w1g_d, FF, 0, W1g_s)
            prep_colmajor(w1u_d, FF, 0, W1u_s)
            prep_rowmajor(w1d_d, NF, None, W1d_s.ap().rearrange("n p c -> p n c"))
            prep_colmajor(win_d, 3072, 2, Win_s, with_v=True)
            prep_rowmajor(wout_d, 8, 3, Wout_s.ap())
            prep_colmajor(w2g_d, FF, 1, W2g_s)
            prep_colmajor(w2u_d, FF, 1, W2u_s)
            prep_rowmajor(w2d_d, NF, None, W2d_s.ap().rearrange("n p c -> p n c"))
            barrier()

        xh = sb("xh", [128, NTB, D])
        xnT = sb("xnT", [128, 8, T], BF16)
        xnb = [sb(f"xnb{i}", [128, D], BF16) for i in range(3)]
        actT = sb("actT", [128, NFQ, T], BF16)
        Wd = [sb(f"Wd{i}", [128, NFQ, D], BF16) for i in range(2)]
        Wst = [sb(f"Wst{i}", [128, 2, 1024], BF16) for i in range(3)]
        sg = [sb(f"sg{i}", [128, 512]) for i in range(2)]
        QT = sb("QT", [128, 4, 2048], BF16)
        KTt = sb("KTt", [128, 4, 2560], BF16)
        V = sb("V", [128, 20, 520], BF16)
        cT = sb("cT", [128, 4, 2048], BF16)
        Ucar = sb("Ucar", [128, 4, 2])
        Bcar = sb("Bcar", [128, 4, 1])
        st8 = sb("st8", [128, 64])
        junk = sb("junk", [128, D], BF16)
        arena = sb("arena", [128, 4224])
        Sb = [arena[:, 0:768], arena[:, 768:1536], arena[:, 1536:2304]]
        PT = [arena[:, 2304:2688].bitcast(BF16), arena[:, 2688:3072].bitcast(BF16), arena[:, 3072:3456].bitcast(BF16)]
        atok = arena[:, 3456:3968]
        abf = arena[:, 3968:4224].bitcast(BF16)
        Csb = arena[:, 0:512]
        Ub = arena[:, 512:1026]
        Bb = arena[:, 1026:1539]
        yb = [arena[:, 1540:2052], arena[:, 3080:3592]]
        rst = arena[:, 2052:2564]
        sqb = [arena[:, 2564:2820].bitcast(BF16), arena[:, 2820:3076].bitcast(BF16)]

        PSALL = es.enter_context(nc.psum_tensor("PSALL", [128, 4096], F32))
        BK = [PSALL[:, i * 512:(i + 1) * 512] for i in range(8)]
        PD = [BK[4], BK[5]]
        TP = [BK[6].bitcast(BF16), BK[7].bitcast(BF16)]

        r_xhh = [[Res(f"xh{t}a"), Res(f"xh{t}b")] for t in range(NTB)]
        r_xh = [None] * NTB
        r_xnT = [Res(f"xnT{t}") for t in range(NTB)]
        r_xnb = [Res(), Res(), Res()]
        r_actT = [[Res(), Res()] for _ in range(NFQ)]
        r_Wd = [Res(), Res()]
        r_Wst = [[Res(), Res()] for _ in range(3)]
        r_sg = [Res(), Res()]
        r_QT, r_KT, r_V, r_cT = Res(), Res(), Res(), Res()
        r_sqbX, r_rst, r_Csb, r_Ub, r_Bb, r_car = Res(), Res(), Res(), Res(), Res(), Res()
        r_yb = [Res(), Res()]
        r_sqb = [Res(), Res()]
        r_Sb = [Res(), Res(), Res()]
        r_PT = [Res(), Res(), Res()]
        r_atok, r_abf, r_st8, r_junk, r_st8b = Res(), Res(), Res(), Res(), Res()
        r_ss = [Res() for _ in range(NTB)]
        r_fs = [Res() for _ in range(NTB)]
        r_BK = [Res(f"bank{i}") for i in range(8)]
        r_PS = [[r_BK[0], r_BK[1]], [r_BK[2], r_BK[3]]]
        r_PD = [r_BK[4], r_BK[5]]
        r_TP = [r_BK[6], r_BK[7]]
        r_hs = [Res() for _ in range(16)]
        PSh = [[BK[0], BK[1]], [BK[2], BK[3]]]
        SLAB = [PSALL[:, 0:1024], PSALL[:, 1024:2048], PSALL[:, 2048:3072]]
        r_SLAB = [[r_BK[0], r_BK[1]], [r_BK[2], r_BK[3]], [r_BK[4], r_BK[5]]]
        banks8 = [(BK[i], r_BK[i]) for i in range(8)]

        op(pool, lambda: nc.gpsimd.memset(V[:].rearrange("p t (h e) -> p t h e", e=65)[:, :, :, 64:65], 1.0), writes=[r_V])

        cnt = dict(tp=0, pd=0, ps=0, ps3=0, yb=0, wst=0, xnb=0, sg=0, ev=0, bk=0, sq=0)

        free_banks = list(range(8))

        def alloc_bank():
            i = free_banks.pop(0)
            return BK[i], r_BK[i], i

        def release_bank(i):
            free_banks.append(i)

        def fence():
            qs = [pe, act, dve]
            toks = [e.cur() for e in qs]
            for e in qs:
                for t in toks:
                    if t is not None and t[0] is not e.sem:
                        e.wait(t)

        def evac_eng():
            cnt["ev"] += 1
            return act if cnt["ev"] % 2 == 0 else dve

        def copy_on(q, out_ap, in_ap, reads, writes):
            if q is act:
                return op(act, lambda: nc.scalar.copy(out=out_ap, in_=in_ap), reads=reads, writes=writes)
            return op(dve, lambda: nc.vector.tensor_copy(out=out_ap, in_=in_ap), reads=reads, writes=writes)

        def rstd_small(ss_ap, out_ap, scale, res):
            op(act, lambda: nc.scalar.activation(out=out_ap, in_=ss_ap, func=AF.Ln, bias=eps_sb[:], scale=scale),
               reads=[res, r_const], writes=[res])
            op(act, lambda: nc.scalar.activation(out=out_ap, in_=out_ap, func=AF.Exp, scale=-0.5),
               reads=[res], writes=[res])

        class PostPipe:
            def __init__(self):
                self.b = []
                self.c = []

            def step(self, stages):
                A, B, C = stages
                if A is not None:
                    A()
                if self.b:
                    f = self.b.pop(0)
                    if f is not None:
                        f()
                self.b.append(B)
                if len(self.c) >= 2:
                    f = self.c.pop(0)
                    if f is not None:
                        f()
                self.c.append(C)

            def flush(self):
                for f in self.b:
                    if f is not None:
                        f()
                self.b = []
                for f in self.c:
                    if f is not None:
                        f()
                self.c = []

        def norm_stages(t):
            box = {}

            def A():
                op(act, lambda: nc.scalar.activation(out=junk[:], in_=xh[:, t, :], func=AF.Square, accum_out=st8[:, t:t + 1]),
                   reads=r_xhh[t], writes=[r_junk, r_ss[t]])
                rstd_small(st8[:, t:t + 1], st8[:, 8 + t:9 + t], 1.0 / D, r_ss[t])

            def B():
                bi = cnt["xnb"] % 3
                cnt["xnb"] += 1
                box["bi"] = bi
                op(dve, lambda: nc.vector.tensor_scalar(out=xnb[bi][:], in0=xh[:, t, :], scalar1=st8[:, 8 + t:9 + t], scalar2=None,
                                                        op0=ALU.mult), reads=r_xhh[t] + [r_ss[t]], writes=[r_xnb[bi]])

            def C():
                bi = box["bi"]
                ti = cnt["tp"] % 2
                cnt["tp"] += 1
                tpv = TP[ti].rearrange("p (k m) -> p k m", m=128)
                pe_group(pe, [lambda k=k: nc.tensor.transpose(tpv[:, k, :], xnb[bi][:, k * 128:(k + 1) * 128], ident[:])
                              for k in range(8)], reads=[r_xnb[bi], r_const], writes=[r_TP[ti]])
                copy_on(evac_eng(), xnT[:, :, t * 128:(t + 1) * 128], tpv, [r_TP[ti]], [r_xnT[t]])
            return A, B, C

        def norm_tiles(ntiles):
            pp = PostPipe()
            for t in range(ntiles):
                pp.step(norm_stages(t))
            pp.flush()

        def wst_load(src_ap):
            slot = (cnt["wst"] // 2) % 3
            hf = cnt["wst"] % 2
            cnt["wst"] += 1
            dma(sp, sw[slot][hf], Wst[slot][:, hf, :], src_ap, writes=[r_Wst[slot][hf]])
            return slot, hf

        pref = []

        def get_tile(key, src_ap):
            if pref:
                k, slot, hf = pref.pop(0)
                assert k == key, (k, key)
                return slot, hf
            return wst_load(src_ap)

        def prefetch_ffn(Wg_s, Wu_s):
            for f in range(2):
                pref.append(((id(Wg_s), f),) + wst_load(Wg_s.ap()[f]))
                pref.append(((id(Wu_s), f),) + wst_load(Wu_s.ap()[f]))

        WIN_ORDER = list(range(0, 8)) + [12, 16, 20, 13, 17, 21, 14, 18, 22, 15, 19, 23]

        def prefetch_win():
            for i in range(4):
                pref.append((("win", i),) + wst_load(Win_s.ap()[WIN_ORDER[i]]))

        def ffn(Wg_s, Wu_s, Wd_s, ntiles, post_tile, after_gateup=None):
            ntok = ntiles * 128
            rx = r_xnT[0:ntiles]
            loads = {}
            pp = PostPipe()

            def issue(f):
                if f < NF and f not in loads:
                    a = get_tile((id(Wg_s), f), Wg_s.ap()[f])
                    b = get_tile((id(Wu_s), f), Wu_s.ap()[f])
                    loads[f] = (a, b)
            issue(0)
            issue(1)
            for qi, (f0, f1) in enumerate(QUARTERS):
                nfq = f1 - f0
                wdi = qi % 2
                dma(sp, swd[wdi], Wd[wdi][:, 0:nfq, :], Wd_s.ap()[f0:f1].rearrange("n p c -> p n c"), writes=[r_Wd[wdi]])
                for f in range(f0, f1):
                    issue(f + 2)
                    (sa, ha), (sb_, hb) = loads.pop(f)
                    fi = f - f0
                    wg = Wst[sa][:, ha, :].rearrange("p (k m) -> p k m", m=128)
                    wu = Wst[sb_][:, hb, :].rearrange("p (k m) -> p k m", m=128)
                    for sbk in range((ntiles + 3) // 4):
                        c0 = sbk * 512
                        c1 = min(c0 + 512, ntok)
                        n = c1 - c0
                        rxs = r_xnT[sbk * 4:min(sbk * 4 + 4, ntiles)]
                        pi = cnt["ps"] % 2
                        cnt["ps"] += 1
                        pe_multi(pe, [
                            ([lambda k=k, pi=pi, c0=c0, c1=c1, n=n: nc.tensor.matmul(PSh[pi][0][:, 0:n], lhsT=wg[:, k, :], rhs=xnT[:, k, c0:c1],
                                                                                   start=(k == 0), stop=(k == 7)) for k in range(8)],
                             [r_Wst[sa][ha]] + rxs, [r_PS[pi][0]]),
                            ([lambda k=k, pi=pi, c0=c0, c1=c1, n=n: nc.tensor.matmul(PSh[pi][1][:, 0:n], lhsT=wu[:, k, :], rhs=xnT[:, k, c0:c1],
                                                                                   start=(k == 0), stop=(k == 7)) for k in range(8)],
                             [r_Wst[sb_][hb]] + rxs, [r_PS[pi][1]])])
                        si = cnt["sg"] % 2
                        cnt["sg"] += 1
                        op(act, lambda pi=pi, si=si, n=n: nc.scalar.activation(out=sg[si][:, 0:n], in_=PSh[pi][0][:, 0:n], func=AF.Silu),
                           reads=[r_PS[pi][0]], writes=[r_sg[si]])
                        op(dve, lambda pi=pi, si=si, fi=fi, c0=c0, c1=c1, n=n: nc.vector.tensor_tensor(out=actT[:, fi, c0:c1], in0=PSh[pi][1][:, 0:n],
                                                                                     in1=sg[si][:, 0:n], op=ALU.mult),
                           reads=[r_PS[pi][1], r_sg[si]], writes=[r_actT[fi][sbk]])
                last = (qi == len(QUARTERS) - 1)
                if last and after_gateup is not None:
                    after_gateup()
                dbanks = [(BK[4], r_BK[4]), (BK[5], r_BK[5])] if last else [(BK[4], r_BK[4]), (BK[5], r_BK[5]), (BK[6], r_BK[6]), (BK[7], r_BK[7])]
                for t in range(ntiles):
                    bsel = []
                    for dh in range(2):
                        bsel.append(dbanks[cnt["pd"] % len(dbanks)])
                        cnt["pd"] += 1
                    groups = [([lambda fi=fi, t=t, dh=dh, pb=bsel[dh][0]: nc.tensor.matmul(
                        pb, lhsT=actT[:, fi, t * 128:(t + 1) * 128], rhs=Wd[wdi][:, fi, dh * 512:(dh + 1) * 512],
                        start=(fi == 0), stop=(fi == nfq - 1)) for fi in range(nfq)],
                        [r_actT[fi][t // 4] for fi in range(nfq)] + [r_Wd[wdi]], [bsel[dh][1]]) for dh in range(2)]
                    if last:
                        for g in groups:
                            pe_group(pe, g[0], reads=g[1], writes=g[2])
                    else:
                        pe_multi(pe, groups)
                    for dh in range(2):
                        pb, rb = bsel[dh]
                        op(dve, lambda t=t, dh=dh, pb=pb: nc.vector.scalar_tensor_tensor(
                            out=xh[:, t, dh * 512:(dh + 1) * 512], in0=pb, scalar=0.5,
                            in1=xh[:, t, dh * 512:(dh + 1) * 512], op0=ALU.mult, op1=ALU.add),
                           reads=[rb, r_xhh[t][dh]], writes=[r_xhh[t][dh]])
                    if last and post_tile is not None:
                        pp.step(post_tile(t))
                if last:
                    pp.flush()

        def grp_sq(src_ap_f32, ncols, src_res):
            k = cnt["sq"] % 2
            cnt["sq"] += 1
            op(act, lambda: nc.scalar.activation(out=sqb[k][:, 0:ncols], in_=src_ap_f32, func=AF.Square),
               reads=[src_res], writes=[r_sqb[k]])
            return k

        def grp_mm(k, ncols):
            pb, rb, bi = alloc_bank()
            pe_group(pe, [lambda: nc.tensor.matmul(pb[:, 0:ncols], lhsT=onesb[:], rhs=sqb[k][:, 0:ncols], start=True, stop=True)],
                     reads=[r_sqb[k], r_const], writes=[rb])
            return pb, rb, bi

        def grp_fin(g, ncols):
            pb, rb, bi = g
            op(act, lambda: nc.scalar.activation(out=rst[:, 0:ncols], in_=pb[:, 0:ncols], func=AF.Ln, bias=eps_sb[:], scale=1.0 / 64),
               reads=[rb, r_const], writes=[r_rst])
            release_bank(bi)
            op(act, lambda: nc.scalar.activation(out=rst[:, 0:ncols], in_=rst[:, 0:ncols], func=AF.Exp, scale=-0.5),
               reads=[r_rst], writes=[r_rst])

        def conv_taps(j, ncols):
            w = cw_sb[:, j * 3:(j + 1) * 3]
            yk = cnt["yb"] % 2
            cnt["yb"] += 1
            y = yb[yk]
            ry = r_yb[yk]
            op(dve, lambda: nc.vector.tensor_scalar(out=y[:, 0:ncols], in0=Ub[:, 0:ncols], scalar1=w[:, 0:1], scalar2=None, op0=ALU.mult),
               reads=[r_Ub, r_const], writes=[ry])
            op(dve, lambda: nc.vector.scalar_tensor_tensor(out=y[:, 0:ncols], in0=Ub[:, 1:ncols + 1], scalar=w[:, 1:2], in1=y[:, 0:ncols],
                                                           op0=ALU.mult, op1=ALU.add), reads=[r_Ub, ry, r_const], writes=[ry])
            op(dve, lambda: nc.vector.scalar_tensor_tensor(out=y[:, 0:ncols], in0=Ub[:, 2:ncols + 2], scalar=w[:, 2:3], in1=y[:, 0:ncols],
                                                           op0=ALU.mult, op1=ALU.add), reads=[r_Ub, ry, r_const], writes=[ry])
            op(dve, lambda: nc.vector.tensor_tensor(out=y[:, 0:ncols], in0=y[:, 0:ncols], in1=Bb[:, 0:ncols], op=ALU.mult),
               reads=[ry, r_Bb], writes=[ry])
            return grp_sq(y[:, 0:ncols], ncols, ry), yk

        def conv_out(j, g, yk, ncols, tok_first, own_lo, own_hi):
            grp_fin(g, ncols)
            lo = max(tok_first, own_lo)
            hi = min(tok_first + ncols, own_hi)
            if hi > lo:
                a, b = lo - tok_first, hi - tok_first
                op(dve, lambda: nc.vector.tensor_tensor(out=cT[:, j, lo - own_lo:hi - own_lo], in0=yb[yk][:, a:b], in1=rst[:, a:b], op=ALU.mult),
                   reads=[r_yb[yk], r_rst], writes=[r_cT])

        def conv_finalize(j, ncols, tok_first, own_lo, own_hi):
            k, yk = conv_taps(j, ncols)
            g = grp_mm(k, ncols)
            conv_out(j, g, yk, ncols, tok_first, own_lo, own_hi)

        def w_in_phase(unit, blk0, ntiles_all, at_end=None):
            nsb = (ntiles_all + 3) // 4
            own_lo = unit["own_off"]
            own_hi = own_lo + 2048
            order = WIN_ORDER * nsb
            loads = {}
            st = dict(pos=0)

            def issue(i):
                if i < len(order):
                    loads[i] = get_tile(("win", i), Win_s.ap()[order[i]])
            for i in range(4):
                issue(i)
            Wv = Wd[1][:].rearrange("p a (b c) -> p (a b) c", c=512)
            r_Wv = r_Wd[1]
            dma(sp, swd[1], Wv, Wv_s.ap(), writes=[r_Wv])

            def make_items(sbk):
                b0 = blk0 + sbk * 512
                cx = sbk * 512
                ntiles = min(4, ntiles_all - 4 * sbk)
                ntok = ntiles * 128
                t0 = cx // 128
                rx = r_xnT[t0:t0 + ntiles]

                def mm8(pb, rb):
                    slot, hf = loads.pop(st["pos"])
                    wt = Wst[slot][:, hf, :].rearrange("p (k m) -> p k m", m=128)
                    pe_group(pe, [lambda k=k: nc.tensor.matmul(pb[:, 0:ntok], lhsT=wt[:, k, :], rhs=xnT[:, k, cx:cx + ntok],
                                                               start=(k == 0), stop=(k == 7)) for k in range(8)],
                             reads=[r_Wst[slot][hf]] + rx, writes=[rb])
                    issue(st["pos"] + 4)
                    st["pos"] += 1

                def qk_item(n):
                    pb, rb, bi = alloc_bank()
                    mm8(pb, rb)
                    pz = pb[:, 0:ntok]
                    box = {}

                    def s2():
                        box["k"] = grp_sq(pz, ntok, rb)

                    def s3():
                        box["g"] = grp_mm(box["k"], ntok)

                    def s4():
                        grp_fin(box["g"], ntok)
                        if n < 4:
                            lo, hi = max(b0, own_lo), min(b0 + ntok, own_hi)
                            if hi > lo:
                                op(dve, lambda: nc.vector.scalar_tensor_tensor(
                                    out=QT[:, n, lo - own_lo:hi - own_lo], in0=pz[:, lo - b0:hi - b0], scalar=gq_sb[:, 0:1],
                                    in1=rst[:, lo - b0:hi - b0], op0=ALU.mult, op1=ALU.mult),
                                   reads=[rb, r_rst, r_const], writes=[r_QT])
                        else:
                            op(dve, lambda: nc.vector.scalar_tensor_tensor(
                                out=KTt[:, n - 4, b0:b0 + ntok], in0=pz, scalar=gk_sb[:, 0:1], in1=rst[:, 0:ntok],
                                op0=ALU.mult, op1=ALU.mult), reads=[rb, r_rst, r_const], writes=[r_KT])
                        release_bank(bi)
                    return s2, s3, s4

                def v_item(t):
                    pb, rb, bi = alloc_bank()
                    tt = t0 + t
                    pe_group(pe, [lambda k=k: nc.tensor.matmul(pb, lhsT=xnT[:, k, tt * 128:(tt + 1) * 128], rhs=Wv[:, k, :],
                                                               start=(k == 0), stop=(k == 7)) for k in range(8)],
                             reads=[r_xnT[tt], r_Wv], writes=[rb])
                    vt = b0 // 128 + t

                    def s2():
                        copy_on(evac_eng(), V[:, vt, :].rearrange("p (h e) -> p h e", e=65)[:, :, 0:64],
                                pb.rearrange("p (h e) -> p h e", e=64), [rb], [r_V])
                        release_bank(bi)
                    return s2, None, None

                def conv_item(j):
                    bks = [alloc_bank() for _ in range(3)]
                    for (pb, rb, _bi) in bks:
                        mm8(pb, rb)
                    (pB, rB, iB), (pC, rC, iC), (pX, rX, iX) = bks
                    box = {}

                    def s2():
                        op(dve, lambda: nc.vector.tensor_copy(out=Ub[:, 0:2], in_=Ucar[:, j, :]), reads=[r_car], writes=[r_Ub])
                        op(dve, lambda: nc.vector.tensor_copy(out=Bb[:, 0:1], in_=Bcar[:, j, :]), reads=[r_car], writes=[r_Bb])
                        op(act, lambda: nc.scalar.copy(out=Csb[:, 0:ntok], in_=pC[:, 0:ntok]), reads=[rC], writes=[r_Csb])
                        op(dve, lambda: nc.vector.tensor_tensor(out=Ub[:, 2:ntok + 2], in0=pX[:, 0:ntok], in1=Csb[:, 0:ntok], op=ALU.mult),
                           reads=[rX, r_Csb], writes=[r_Ub])
                        op(act, lambda: nc.scalar.copy(out=Bb[:, 1:ntok + 1], in_=pB[:, 0:ntok]), reads=[rB], writes=[r_Bb])
                        op(dve, lambda: nc.vector.tensor_copy(out=Ucar[:, j, :], in_=Ub[:, ntok:ntok + 2]), reads=[r_Ub], writes=[r_car])
                        op(dve, lambda: nc.vector.tensor_copy(out=Bcar[:, j, :], in_=Bb[:, ntok:ntok + 1]), reads=[r_Bb], writes=[r_car])
                        for _i in (iB, iC, iX):
                            release_bank(_i)
                        box["k"], box["yk"] = conv_taps(j, ntok)

                    def s3():
                        box["g"] = grp_mm(box["k"], ntok)

                    def s4():
                        conv_out(j, box["g"], box["yk"], ntok, b0 - 1, own_lo, own_hi)
                    return s2, s3, s4


                items = [(qk_item, n) for n in range(8)]
                for j in range(4):
                    items.append((conv_item, j))
                    if j < ntiles:
                        items.append((v_item, j))
                return items

            items = []
            for sbk in range(nsb):
                items += make_items(sbk)
            hist = []
            for ii, (fn, a) in enumerate(items + [(None, None)] * 3):
                if fn is not None:
                    hist.append(fn(a))
                else:
                    hist.append((None, None, None))
                if ii == len(items) - 1 and at_end is not None:
                    at_end()
                if len(hist) >= 3 and hist[-3][1] is not None:
                    hist[-3][1]()
                if len(hist) >= 2 and hist[-2][0] is not None:
                    hist[-2][0]()
                if len(hist) >= 4 and hist[-4][2] is not None:
                    hist[-4][2]()

        def conv_flush(unit):
            own_lo = unit["own_off"]
            own_hi = own_lo + 2048
            last = unit["n_ext"] - 1
            if not (own_lo <= last < own_hi):
                return
            for j in range(4):
                op(dve, lambda j=j: nc.vector.tensor_copy(out=Ub[:, 0:2], in_=Ucar[:, j, :]), reads=[r_car], writes=[r_Ub])
                op(dve, lambda: nc.vector.memset(Ub[:, 2:4], 0.0), writes=[r_Ub])
                op(dve, lambda j=j: nc.vector.tensor_copy(out=Bb[:, 0:1], in_=Bcar[:, j, :]), reads=[r_car], writes=[r_Bb])
                op(dve, lambda: nc.vector.memset(Bb[:, 1:2], 0.0), writes=[r_Bb])
                conv_finalize(j, 2, last, own_lo, own_hi)

        def attention_block(unit, q0):
            kind = unit["kind"]
            seq = [(tl, h) for tl in range(NTB) for h in range(8)]

            def scores(tl, h):
                plan = plans[kind][q0 + tl]
                combos, runs, y0 = plan["combos"], plan["runs"], plan["y0"]
                q = q0 + tl
                ncol = len(combos) * 128
                pi = cnt["ps3"] % 3
                cnt["ps3"] += 1
                hp, hf = h // 2, h % 2
                ps = SLAB[pi]
                pe_group(pe, [lambda ci=ci, kp=kp: nc.tensor.matmul(
                    ps[:, ci * 128:(ci + 1) * 128], lhsT=KTt[hf * 64:(hf + 1) * 64, hp, kp * 128:(kp + 1) * 128],
                    rhs=QT[hf * 64:(hf + 1) * 64, hp, q * 128:(q + 1) * 128], start=True, stop=True)
                    for ci, kp in enumerate(combos)],
                    reads=[r_KT, r_QT], writes=r_SLAB[pi])
                off = (y0 - 2) * 64
                op(dve, lambda: nc.vector.tensor_tensor(out=Sb[pi][:, 0:ncol], in0=ps[:, 0:ncol], in1=TT3[:, h, off:off + ncol], op=ALU.add),
                   reads=r_SLAB[pi] + [r_const], writes=[r_Sb[pi]])
                for (c0, c1, mc) in runs:
                    if mc == 0:
                        op(act, lambda c0=c0, c1=c1: nc.scalar.activation(out=PT[pi][:, c0:c1], in_=Sb[pi][:, c0:c1], func=AF.Exp),
                           reads=[r_Sb[pi]], writes=[r_PT[pi]])
                    elif mc == 3:
                        op(dve, lambda c0=c0, c1=c1: nc.vector.memset(PT[pi][:, c0:c1], 0.0), writes=[r_PT[pi]])
                    else:
                        op(act, lambda c0=c0, c1=c1, mc=mc: nc.scalar.activation(out=PT[pi][:, c0:c1], in_=Sb[pi][:, c0:c1], func=AF.Exp,
                                                                                 bias=maskt[:, mc:mc + 1]),
                           reads=[r_Sb[pi], r_const], writes=[r_PT[pi]])
                return pi

            def pv(tl, h, pi):
                combos = plans[kind][q0 + tl]["combos"]
                nco = len(combos)
                o = BK[6][:, h * 65:h * 65 + 65] if h < 7 else BK[7][:, 0:65]
                first = (h == 0 or h == 7)
                rb = r_BK[6] if h < 7 else r_BK[7]
                pe_group(pe, [lambda ci=ci, kp=kp: nc.tensor.matmul(
                    o, lhsT=PT[pi][:, ci * 128:(ci + 1) * 128], rhs=V[:, kp, h * 65:(h + 1) * 65],
                    start=(ci == 0), stop=(ci == nco - 1)) for ci, kp in enumerate(combos)],
                    reads=[r_PT[pi], r_V], writes=[rb] if first else [])
                if not first:
                    rb.w = pe.cur()

            def tail_s1(tl):
                for (bk, h0, nh, c0) in ((6, 0, 7, 16), (7, 7, 1, 23)):
                    ov = BK[bk][:, 0:nh * 65].rearrange("p (h e) -> p h e", e=65)
                    op(dve, lambda ov=ov, c0=c0, nh=nh: nc.vector.reciprocal(out=st8[:, c0:c0 + nh].unsqueeze(2), in_=ov[:, :, 64:65]),
                       reads=[r_BK[bk]], writes=[r_st8])
                    op(dve, lambda ov=ov, c0=c0, nh=nh, h0=h0: nc.vector.tensor_tensor(
                        out=atok[:, h0 * 64:(h0 + nh) * 64].rearrange("p (h e) -> p h e", e=64), in0=ov[:, :, 0:64],
                        in1=st8[:, c0:c0 + nh].unsqueeze(2).broadcast_to([128, nh, 64]), op=ALU.mult),
                       reads=[r_BK[bk], r_st8], writes=[r_atok])
                op(act, lambda: nc.scalar.activation(out=junk[:, 0:512], in_=atok, func=AF.Square), reads=[r_atok], writes=[r_junk])

            def tail_s2(tl):
                op(dve, lambda: nc.vector.tensor_reduce(out=st8[:, 24:32], in_=junk[:, 0:512].rearrange("p (h e) -> p h e", e=64),
                                                        axis=AX.X, op=ALU.add), reads=[r_junk], writes=[r_st8b])
                rstd_small(st8[:, 24:32], st8[:, 32:40], 1.0 / 64, r_st8b)

            def tail_s3(tl):
                op(dve, lambda: nc.vector.tensor_tensor(out=abf.rearrange("p (h e) -> p h e", e=64),
                                                        in0=atok.rearrange("p (h e) -> p h e", e=64),
                                                        in1=st8[:, 32:40].unsqueeze(2).broadcast_to([128, 8, 64]), op=ALU.mult),
                   reads=[r_atok, r_st8b], writes=[r_abf])

            def tail_pe(tl):
                tpv = TP[1][:, 512:1024].rearrange("p (k m) -> p k m", m=128)
                pe_group(pe, [lambda k=k: nc.tensor.transpose(tpv[:, k, :], abf[:, k * 128:(k + 1) * 128], ident[:]) for k in range(4)],
                         reads=[r_abf, r_const], writes=[r_BK[7]])
                copy_on(evac_eng(), xnT[:, 0:4, tl * 128:(tl + 1) * 128], tpv, [r_BK[7]], [r_xnT[tl]])

            LOOK = 2
            slabs = {}
            todo = []
            for i in range(min(LOOK, len(seq))):
                slabs[i] = scores(*seq[i])
            for i, (tl, h) in enumerate(seq):
                if i + LOOK < len(seq):
                    slabs[i + LOOK] = scores(*seq[i + LOOK])
                pv(tl, h, slabs.pop(i))
                while todo and todo[0][0] <= i:
                    _, fn, a = todo.pop(0)
                    fn(a)
                if h == 7:
                    tail_s1(tl)
                    todo += [(i + 1, tail_s2, tl), (i + 2, tail_s3, tl), (i + 4, tail_pe, tl)]
            for _, fn, a in todo:
                fn(a)

        for unit in units:
            n_ext = unit["n_ext"]
            own_lo = unit["own_off"]
            own_hi = own_lo + 2048
            fence()
            op(dve, lambda: nc.vector.memset(Ucar[:], 0.0), writes=[r_car])
            op(dve, lambda: nc.vector.memset(Bcar[:], 0.0), writes=[r_car])
            blocks_a = list(range(0, n_ext, T))
            for b0 in blocks_a:
                ntiles = min(NTB, (n_ext - b0) // 128)
                for t in range(ntiles):
                    r0 = unit["in0"] + b0 + t * 128
                    dma(sp, sx[t], xh[:, t, :], xin[r0:r0 + 128, :], writes=r_xhh[t])
                norm_tiles(ntiles)

                def post_a(t, b0=b0):
                    A, B, C = norm_stages(t)

                    def A2():
                        A()
                        tok = b0 + t * 128
                        if own_lo <= tok < own_hi:
                            ot = (tok - own_lo) // 128
                            dma(pool, sxs[t], h_s.ap()[ot * 128:(ot + 1) * 128, :], xh[:, t, :], reads=r_xhh[t], writes=[r_hs[ot]])
                    return A2, B, C
                ffn(W1g_s, W1u_s, W1d_s, ntiles, post_a, after_gateup=prefetch_win)
                w_in_phase(unit, b0, ntiles,
                           at_end=None if b0 == blocks_a[-1] else (lambda: prefetch_ffn(W1g_s, W1u_s)))
            conv_flush(unit)
            fence()
            for ob in range(0, 2048, T):
                for t in range(NTB):
                    ot = ob // 128 + t
                    dma(sp, sx[t], xh[:, t, :], h_s.ap()[ot * 128:(ot + 1) * 128, :], reads=[r_hs[ot]], writes=r_xhh[t])
                dma(sp, swd[0], Wd[0][:, 0:4, :], Wout_s.ap()[:, 0:4, :], writes=[r_Wd[0]])
                dma(sp, swd[1], Wd[1][:, 0:4, :], Wout_s.ap()[:, 4:8, :], writes=[r_Wd[1]])
                prefetch_ffn(W2g_s, W2u_s)
                attention_block(unit, ob // 128)
                wpp = PostPipe()
                for t in range(NTB):
                    ot = ob // 128 + t
                    for dh in range(2):
                        di = cnt["pd"] % 2
                        cnt["pd"] += 1
                        fns = []
                        for k in range(8):
                            if k < 4:
                                fns.append(lambda k=k, t=t, dh=dh, di=di: nc.tensor.matmul(
                                    PD[di], lhsT=xnT[:, k, t * 128:(t + 1) * 128], rhs=Wd[0][:, k, dh * 512:(dh + 1) * 512],
                                    start=(k == 0), stop=False))
                            else:
                                fns.append(lambda k=k, ot=ot, dh=dh, di=di: nc.tensor.matmul(
                                    PD[di], lhsT=cT[:, k - 4, ot * 128:(ot + 1) * 128], rhs=Wd[1][:, k - 4, dh * 512:(dh + 1) * 512],
                                    start=False, stop=(k == 7)))
                        pe_group(pe, fns, reads=[r_xnT[t], r_cT, r_Wd[0], r_Wd[1]], writes=[r_PD[di]])
                        op(dve, lambda t=t, dh=dh, di=di: nc.vector.tensor_tensor(
                            out=xh[:, t, dh * 512:(dh + 1) * 512], in0=PD[di], in1=xh[:, t, dh * 512:(dh + 1) * 512], op=ALU.add),
                           reads=[r_PD[di], r_xhh[t][dh]], writes=[r_xhh[t][dh]])
                    wpp.step(norm_stages(t))
                wpp.flush()

                def post_b(t, ob=ob):
                    def A():
                        op(act, lambda: nc.scalar.activation(out=junk[:], in_=xh[:, t, :], func=AF.Square, accum_out=st8[:, 40 + t:41 + t]),
                           reads=r_xhh[t], writes=[r_junk, r_fs[t]])
                        rstd_small(st8[:, 40 + t:41 + t], st8[:, 48 + t:49 + t], 1.0 / D, r_fs[t])

                    def B():
                        op(dve, lambda: nc.vector.scalar_tensor_tensor(out=xh[:, t, :], in0=xh[:, t, :], scalar=st8[:, 48 + t:49 + t],
                                                                       in1=gfin_sb[:], op0=ALU.mult, op1=ALU.mult),
                           reads=r_xhh[t] + [r_fs[t], r_const], writes=r_xhh[t])
                        r0 = unit["y0"] + ob + t * 128
                        dma(pool, sxs[t], yout[r0:r0 + 128, :], xh[:, t, :], reads=r_xhh[t])
                    return A, B, None
                nxt_unit = (ob + T >= 2048) and (unit is not units[-1])
                ffn(W2g_s, W2u_s, W2d_s, NTB, post_b,
                    after_gateup=(lambda: prefetch_ffn(W1g_s, W1u_s)) if nxt_unit else None)
        for e in (sp, pool):
            for s in streams:
                e.wait(s.cur())
    return nc, dict(NTOK_IN=NTOK_IN, NTOK_OUT=NTOK_OUT, NM=NM, dyn_cols=dyn_cols, units=units)


def _tile8(g):
    return np.ascontiguousarray(np.asarray(g, np.float32).reshape(8, 128).T)


def common_inputs(p, dyn_cols, variant):
    onesblk = np.zeros((128, 128), np.float32)
    onesblk[:64, :64] = 1.0
    onesblk[64:, 64:] = 1.0
    cols = np.arange(64)
    cs = np.clip(cols - 8, 0, 48)
    colmask = np.where((cols[:, None] >= cs[None, :]) & (cols[:, None] < cs[None, :] + 16), 0.0, NEG).astype(np.float32)
    colmask = np.concatenate([colmask, colmask], 0)
    rpb = np.asarray(p["rel_pos_bias"], np.float32)[0]
    d = dict(
        g1=_tile8(p["ffn1_norm"][0]), g2=_tile8(p["ffn2_norm"][0]), gm=_tile8(p["mix_norm"][0]),
        go=_tile8(np.concatenate([np.asarray(p["attn_out_norm"][0]), np.asarray(p["conv_out_norm"][0])])),
        gq=np.ascontiguousarray(np.tile(np.asarray(p["q_norm"][0], np.float32), 2)[:, None]),
        gk=np.ascontiguousarray(np.tile(np.asarray(p["k_norm"][0], np.float32), 2)[:, None]),
        convw=np.ascontiguousarray(np.asarray(p["conv_w"][0], np.float32).reshape(3, 4, 128).transpose(2, 1, 0).reshape(128, 12)),
        gfin=np.ascontiguousarray(np.broadcast_to(np.asarray(p["final_norm"][0], np.float32)[None, :], (128, D))),
        w1g=np.asarray(p["ffn1_w_gate"][0], np.float32), w1u=np.asarray(p["ffn1_w_up"][0], np.float32),
        w1d=np.asarray(p["ffn1_w_down"][0], np.float32), win=np.asarray(p["w_in"][0], np.float32),
        wout=np.asarray(p["w_out"][0], np.float32), w2g=np.asarray(p["ffn2_w_gate"][0], np.float32),
        w2u=np.asarray(p["ffn2_w_up"][0], np.float32), w2d=np.asarray(p["ffn2_w_down"][0], np.float32),
        rpb_rev=np.ascontiguousarray(rpb[:, :, ::-1].reshape(120, 31)),
        ident=np.eye(128, dtype=np.float32), onesblk=onesblk, colmask=colmask,
        maskt=mask_table(dyn_cols, variant),
        mask01=(mask_table(dyn_cols, variant) == 0).astype(np.float32),
    )
    return d


def sample_ext(xs_seq, half):
    ext = np.zeros((2560, D), np.float32)
    if half == 0:
        ext[256:2560] = xs_seq[0:2304]
    else:
        ext[0:2304] = xs_seq[2048 - 256:4096]
    return ext


_CACHE = {}


def kernel(**inputs):
    x_prompt = np.asarray(inputs["x_prompt"], np.float32)
    x_sample = np.asarray(inputs["x_sample"], np.float32)
    if "prog" not in _CACHE:
        _CACHE["prog"] = build_program(4, True)
    nc, meta = _CACHE["prog"]
    in_maps = []
    for c in range(8):
        half = c % 2
        d = common_inputs(inputs, meta["dyn_cols"], half)
        xin = np.concatenate([x_prompt[4 * c:4 * c + 4].reshape(4 * 2048, D), sample_ext(x_sample[c // 2], half)], 0)
        d["xin"] = np.ascontiguousarray(xin)
        in_maps.append(d)
    res = run_bass_kernel_spmd(nc, in_maps, core_ids=list(range(8)))
    y_prompt = np.empty((32, 2048, D), np.float32)
    y_sample = np.empty((4, 4096, D), np.float32)
    for c in range(8):
        y = res.results[c]["yout"]
        y_prompt[4 * c:4 * c + 4] = y[0:8192].reshape(4, 2048, D)
        half = c % 2
        y_sample[c // 2, half * 2048:(half + 1) * 2048] = y[8192:10240]
    return (y_prompt, y_sample)
```
